# Optimizing a Trainium2 kernel written in Bass

```python
import jax, jax.numpy as jnp
from jax import lax
import numpy as np

D_MODEL = 1024
BATCH = 2
SEQ = 8192
DEPTH = 1

D_MIX = D_MODEL
ATTN_WIDTH = D_MIX // 2
N_HEADS = 8
HEAD_DIM = ATTN_WIDTH // N_HEADS
DILATED_PATTERNS = ((128, 1), (512, 4), (2048, 16))
BLOCK = 128
POOL_WIDTH = D_MIX - ATTN_WIDTH
POOL_WINDOWS = (2, 4, 8, 16)
N_POOL_GROUPS = len(POOL_WINDOWS)
POOL_GROUP_DIM = POOL_WIDTH // N_POOL_GROUPS
D_FF = 2816
CONV_WIDTH = 3
EPS = 1e-6
NEG_INF = -1e30

kernel_name = "hybrid_dilated_attn_pool_convffn_sandwich"


def rms_norm(x, g):
    x32 = x.astype(jnp.float32)
    y = x32 * lax.rsqrt(jnp.mean(x32 * x32, axis=-1, keepdims=True) + EPS)
    return (y * g.astype(jnp.float32)).astype(x.dtype)


def dilated_window_attention(q, k, v, window, dilation):
    B, H, S, hd = q.shape
    span = window // dilation
    L = S // dilation
    nb = -(-L // BLOCK)
    Lp = nb * BLOCK

    def to_res(a):
        return a.reshape(B, H, L, dilation, hd).transpose(0, 1, 3, 2, 4)

    lead = ((0, 0), (0, 0), (0, 0))
    qb = jnp.pad(to_res(q), lead + ((0, Lp - L), (0, 0))).reshape(B, H, dilation, nb, BLOCK, hd)

    def key_blocks(a):
        ap = jnp.pad(to_res(a), lead + ((BLOCK, Lp - L), (0, 0)))
        prev = ap[:, :, :, :Lp].reshape(B, H, dilation, nb, BLOCK, hd)
        cur = ap[:, :, :, BLOCK:].reshape(B, H, dilation, nb, BLOCK, hd)
        return jnp.concatenate([prev, cur], axis=4)

    kb = key_blocks(k)
    vb = key_blocks(v)
    scores = jnp.einsum('bhrnqd,bhrnkd->bhrnqk', qb.astype(jnp.float32),
                        kb.astype(jnp.float32)) * (hd ** -0.5)
    qi = jnp.arange(BLOCK)[:, None]
    ki = jnp.arange(2 * BLOCK)[None, :]
    blk = jnp.arange(nb)[:, None, None]
    dist = qi + BLOCK - ki
    key_pos = blk * BLOCK - BLOCK + ki
    mask = (dist >= 0) & (dist <= span) & (key_pos >= 0)
    scores = jnp.where(mask, scores, NEG_INF)
    m = jnp.max(scores, axis=-1, keepdims=True)
    p = jnp.exp(scores - m)
    denom = jnp.sum(p, axis=-1, keepdims=True)
    out = jnp.einsum('bhrnqk,bhrnkd->bhrnqd', p, vb.astype(jnp.float32)) / denom
    lse = (m + jnp.log(denom))[..., 0]
    out = out.reshape(B, H, dilation, Lp, hd)[:, :, :, :L]
    out = out.transpose(0, 1, 3, 2, 4).reshape(B, H, S, hd)
    lse = lse.reshape(B, H, dilation, Lp)[..., :L].transpose(0, 1, 3, 2).reshape(B, H, S)
    return out, lse


def dilated_mixture_attention(q, k, v):
    outs, lses = [], []
    for window, dilation in DILATED_PATTERNS:
        o, l = dilated_window_attention(q, k, v, window, dilation)
        outs.append(o)
        lses.append(l)
    w = jax.nn.softmax(jnp.stack(lses, axis=0), axis=0)
    return jnp.sum(w[..., None] * jnp.stack(outs, axis=0), axis=0)


def multiscale_pool_mixer(u, pool_w, pool_scale):
    B, S, _ = u.shape
    ug = u.astype(jnp.float32).reshape(B, S, N_POOL_GROUPS, POOL_GROUP_DIM)
    t = jnp.arange(S)
    outs = []
    for g, w in enumerate(POOL_WINDOWS):
        xg = ug[:, :, g]
        csum = jnp.cumsum(xg, axis=1)
        lagged = jnp.pad(csum, ((0, 0), (w, 0), (0, 0)))[:, :S]
        count = jnp.minimum(t + 1, w).astype(jnp.float32)[None, :, None]
        pooled = (csum - lagged) / count - xg
        outs.append(jnp.einsum('bsc,cd->bsd', pooled, pool_w[g].astype(jnp.float32)))
    y = jnp.concatenate(outs, axis=-1) * pool_scale.astype(jnp.float32)
    return y.astype(u.dtype)


def conv_gated_mlp(h, w_up, conv_w, conv_b, w_down):
    S = h.shape[1]
    u = jnp.einsum('bsd,df->bsf', h, w_up)
    up = jnp.pad(u, ((0, 0), (CONV_WIDTH - 1, 0), (0, 0)))
    c = conv_b + sum(up[:, j:j + S] * conv_w[j] for j in range(CONV_WIDTH))
    gate, val = jnp.split(c, 2, axis=-1)
    y = jax.nn.gelu(gate, approximate=True) * val
    return jnp.einsum('bsf,fd->bsd', y, w_down)


def setup_inputs(seed: int = 0) -> dict:
    key = jax.random.key(seed)
    ks = jax.random.split(key, 16)
    f32 = jnp.float32

    def nrm(k, shape, scale):
        return jax.random.normal(k, shape, f32) * scale

    def gain(k, n):
        return 1.0 + 0.05 * jax.random.normal(k, (DEPTH, n), f32)

    n_in = 3 * ATTN_WIDTH + POOL_WIDTH
    return {
        "x": jax.random.normal(ks[0], (BATCH, SEQ, D_MODEL), f32),
        "g_mix_pre": gain(ks[1], D_MODEL),
        "w_in": nrm(ks[2], (DEPTH, D_MODEL, n_in), D_MODEL ** -0.5),
        "pool_w": nrm(ks[3], (DEPTH, N_POOL_GROUPS, POOL_GROUP_DIM, POOL_GROUP_DIM), POOL_GROUP_DIM ** -0.5),
        "pool_scale": 1.0 + 0.1 * jax.random.normal(ks[4], (DEPTH, POOL_WIDTH), f32),
        "w_out": nrm(ks[5], (DEPTH, D_MIX, D_MODEL), D_MIX ** -0.5),
        "g_mix_post": gain(ks[6], D_MODEL),
        "g_ffn_pre": gain(ks[7], D_MODEL),
        "w_up": nrm(ks[8], (DEPTH, D_MODEL, 2 * D_FF), D_MODEL ** -0.5),
        "conv_w": nrm(ks[9], (DEPTH, CONV_WIDTH, 2 * D_FF), CONV_WIDTH ** -0.5),
        "conv_b": nrm(ks[10], (DEPTH, 2 * D_FF), 0.01),
        "w_down": nrm(ks[11], (DEPTH, D_FF, D_MODEL), D_FF ** -0.5),
        "g_ffn_post": gain(ks[12], D_MODEL),
    }


def reference(x, g_mix_pre, w_in, pool_w, pool_scale, w_out, g_mix_post,
              g_ffn_pre, w_up, conv_w, conv_b, w_down, g_ffn_post):
    B, S, _ = x.shape
    for layer in range(DEPTH):
        h = rms_norm(x, g_mix_pre[layer])
        proj = jnp.einsum('bsd,dn->bsn', h, w_in[layer])
        q, k, v, pool_in = jnp.split(
            proj, [ATTN_WIDTH, 2 * ATTN_WIDTH, 3 * ATTN_WIDTH], axis=-1)

        def heads(a):
            return a.reshape(B, S, N_HEADS, HEAD_DIM).transpose(0, 2, 1, 3)

        attn = dilated_mixture_attention(heads(q), heads(k), heads(v))
        attn = attn.transpose(0, 2, 1, 3).reshape(B, S, ATTN_WIDTH).astype(x.dtype)
        pool = multiscale_pool_mixer(pool_in, pool_w[layer], pool_scale[layer])
        mixed = jnp.einsum('bsm,md->bsd', jnp.concatenate([attn, pool], axis=-1), w_out[layer])
        x = x + rms_norm(mixed, g_mix_post[layer])
        h = rms_norm(x, g_ffn_pre[layer])
        f = conv_gated_mlp(h, w_up[layer], conv_w[layer], conv_b[layer], w_down[layer])
        x = x + rms_norm(f, g_ffn_post[layer])
    return x
```

```python
import numpy as np
from contextlib import ExitStack
import concourse.bass as bass
import concourse.mybir as mybir
from concourse.bass_utils import run_bass_kernel_spmd

F32 = mybir.dt.float32
BF16 = mybir.dt.bfloat16
AF = mybir.ActivationFunctionType
ALU = mybir.AluOpType

D = 1024
SEQ = 8192
NCORES = 8
OWN = 2048
TK = 4224
NT = TK // 128
OWN0 = TK - OWN
NQ = 2176
QOFF = 128
DFF = 2816
NFC = 22
EPS = 1e-6

C_ID = 0
C_MASK = C_ID + 128
C_ME = C_MASK + 1536
CSTB_COLS = C_ME + 34
C_SELA = 0
C_SELB = C_SELA + 64
C_INVC = C_SELB + 128
CST_COLS = C_INVC + 64
P_G1 = 0
P_G2 = 8
P_PS = 16
P_CW = 20
P_CB = P_CW + 132
P_CW2 = P_CB + 44
PP_COLS = P_CW2 + 132


class Res:
    __slots__ = ("name", "w", "r", "dsem", "dcnt", "psum")

    def __init__(self, name, after=None, psum=False):
        self.name = name
        self.psum = psum
        self.w = None
        self.r = dict(after) if after else {}
        self.dsem = None
        self.dcnt = 0


class Sched:
    ENG = ("pe", "act", "dve", "pool", "sp")
    HANDLES = {"pe": "tensor", "act": "scalar", "dve": "vector", "pool": "gpsimd", "sp": "sync"}

    def __init__(self, nc, stack):
        self.nc = nc
        self.stack = stack
        self.sems = {}
        self.cnt = {}
        self.waited = {k: {} for k in self.ENG}
        self.pending = {k: False for k in self.ENG}
        self.dlatest = {}
        for k in self.ENG:
            self.sems[k] = stack.enter_context(nc.semaphore("s_" + k))
            self.cnt[k] = 0
        self.nd = 0
        self.ninst = 0

    def new_dsem(self):
        self.nd += 1
        key = "d%d" % self.nd
        self.sems[key] = self.stack.enter_context(self.nc.semaphore("s_" + key))
        return key

    def snapshot(self):
        snap = {}
        for k in self.ENG:
            v = self.cnt[k] + (1 if self.pending[k] else 0)
            if v:
                snap[k] = v
        snap.update(self.dlatest)
        return snap

    def _deps(self, E, reads, writes):
        best = {}
        for r in reads:
            if r.w is not None:
                k, v = r.w
                if not (k == E and E == "pe"):
                    best[k] = max(best.get(k, 0), v)
            if r.psum:
                for k, v in r.r.items():
                    if k != E:
                        best[k] = max(best.get(k, 0), v)
        for w in writes:
            if w.w is not None:
                k, v = w.w
                if not (k == E and E == "pe"):
                    best[k] = max(best.get(k, 0), v)
            for k, v in w.r.items():
                if not (k == E and E == "pe"):
                    best[k] = max(best.get(k, 0), v)
        out = []
        for k, v in best.items():
            if self.waited[E].get(k, 0) < v:
                self.waited[E][k] = v
                out.append((k, v))
        return out

    def _emit(self, E, waits, fn, inc):
        eng = getattr(self.nc, self.HANDLES[E])
        for k, v in waits:
            eng.wait_ge(self.sems[k], v)
        if fn is not None:
            ins = fn(eng)
            self.ninst += 1
            if inc is not None:
                ins.then_inc(self.sems[inc[0]], inc[1])

    def op(self, E, fn, reads=(), writes=(), signal=True):
        waits = self._deps(E, reads, writes)
        if signal:
            self.cnt[E] += 1
            tok = (E, self.cnt[E])
            self.pending[E] = False
        else:
            tok = (E, self.cnt[E] + 1)
            self.pending[E] = True
        self._emit(E, waits, fn, (E, 1) if signal else None)
        for r in reads:
            r.r[tok[0]] = max(r.r.get(tok[0], 0), tok[1])
        for w in writes:
            w.w = tok
            w.r = {}
        return tok

    def dma(self, E, fn, reads=(), writes=(), sem_owner=None):
        waits = self._deps(E, reads, writes)
        own = sem_owner
        if own.dsem is None:
            own.dsem = self.new_dsem()
        own.dcnt += 16
        tok = (own.dsem, own.dcnt)
        self.dlatest[own.dsem] = own.dcnt
        self._emit(E, waits, fn, (own.dsem, 16))
        for r in reads:
            r.r[tok[0]] = max(r.r.get(tok[0], 0), tok[1])
        for w in writes:
            w.w = tok
            w.r = {}
        return tok

    def wait_tokens(self, E, toks):
        waits = []
        for k, v in toks:
            if self.waited[E].get(k, 0) < v:
                self.waited[E][k] = v
                waits.append((k, v))
        if waits:
            self._emit(E, waits, None, None)


def _blocks():
    blk = []
    idx = {}
    for p, d in ((0, 1), (1, 4), (2, 16)):
        for r in range(d):
            for m in range(-1, 16 // d):
                s = OWN0 + 128 * d * m + r
                idx[(p, r, m)] = len(blk)
                blk.append(slice(s, s + 127 * d + 1, d))
    idx["hh"] = len(blk)
    blk.append(slice(0, 128, 1))
    return blk, idx


def build_nc(dbg=None):
    nc = bass.Bass("TRN2", target_bir_lowering=False)

    def din(name, shape, dt=F32):
        return nc.dram_tensor(name, shape, dt, kind="ExternalInput").ap()

    xk = din("xk", [TK, D])
    cst = din("cst", [128, CST_COLS])
    cstb = din("cstb", [128, CSTB_COLS])
    pp = din("pp", [128, PP_COLS])
    w_in = din("w_in", [D, 2048])
    pool_w = din("pool_w", [512, 128])
    w_out = din("w_out", [D, D])
    w_up = din("w_up", [D, 2 * DFF])
    w_down = din("w_down", [DFF, D])
    g_post = din("g_mix_post", [D])
    g_fpost = din("g_ffn_post", [D])
    out = nc.dram_tensor("out", [OWN, D], F32, kind="ExternalOutput").ap()
    dbg_out = {}
    if dbg:
        for name, (shape, dt) in dbg.items():
            dbg_out[name] = nc.dram_tensor("dbg_" + name, shape, dt, kind="ExternalOutput").ap()

    wup_bf = nc.dram_tensor("wup_bf", [D, 2 * DFF], BF16, kind="Internal").ap()
    wup_bf_v = wup_bf.rearrange("(k p) n -> p k n", p=128)
    wout_bf = nc.dram_tensor("wout_bf", [D, D], BF16, kind="Internal").ap()
    wout_bf_v = wout_bf.rearrange("(k p) n -> p k n", p=128)
    wdn_bf = nc.dram_tensor("wdn_bf", [DFF, D], BF16, kind="Internal").ap()
    wdn_bf_v = wdn_bf.rearrange("(k p) n -> p k n", p=128)
    plw_bf = nc.dram_tensor("plw_bf", [512, 128], BF16, kind="Internal").ap()
    plw_bf_v = plw_bf.rearrange("(g p) n -> p g n", p=128)
    w_in_v = w_in.rearrange("(k p) n -> p k n", p=128)
    w_out_v = w_out.rearrange("(k p) n -> p k n", p=128)
    w_up_v = w_up.rearrange("(k p) n -> p k n", p=128)
    w_down_v = w_down.rearrange("(k p) n -> p k n", p=128)
    pool_w_v = pool_w.rearrange("(g p) n -> p g n", p=128)

    blocks, bidx = _blocks()
    NB = len(blocks)

    with ExitStack() as gst:
        S = Sched(nc, gst)

        def T(st, name, shape, dt):
            return st.enter_context(nc.sbuf_tensor(name, shape, dt))

        def PS(st, name, shape, dt):
            return st.enter_context(nc.psum_tensor(name, shape, dt))

        r_out = Res("out")
        r_dbg = Res("dbg")

        cstf = T(gst, "cstf", [128, CST_COLS], F32); r_cstf = Res("cstf")
        ppt = T(gst, "ppt", [128, PP_COLS], F32); r_pp = Res("pp")
        ident = T(gst, "ident", [128, 128], BF16); r_ident = Res("ident")
        gB2 = T(gst, "gB2", [128, 8, 128], BF16); r_gB2 = Res("gB2")
        onesb = T(gst, "onesb", [128, 128], BF16); r_ones = Res("ones")
        epst = T(gst, "epst", [128, 1], F32); r_eps = Res("eps")
        stats = T(gst, "stats", [128, 3, 128], F32)
        junks = [T(gst, "junk%d" % i, [128, D], BF16) for i in range(2)]
        r_junks = [Res("junk%d" % i) for i in range(2)]
        mixinT = T(gst, "mixinT", [128, 8, NQ], BF16)
        r_mix = [[Res("mix%d_%d" % (k, j)) for j in range(5)] for k in range(8)]

        r_wupbf = Res("wupbf")
        r_woutbf = Res("woutbf")
        r_wdnbf = Res("wdnbf")
        r_plwbf = Res("plwbf")
        S.dma("sp", lambda e: e.dma_start(out=cstf[:], in_=cst), writes=[r_cstf], sem_owner=r_cstf)
        S.dma("sp", lambda e: e.dma_start(out=ppt[:], in_=pp), writes=[r_pp], sem_owner=r_pp)
        S.dma("pool", lambda e: e.dma_start(out=ident[:], in_=cstb[:, C_ID:C_ID + 128]), writes=[r_ident], sem_owner=r_ident)
        S.op("dve", lambda e: e.memset(onesb[:], 1.0), writes=[r_ones])
        S.op("dve", lambda e: e.memset(epst[:], EPS), writes=[r_eps])
        S.op("dve", lambda e: e.memset(mixinT[:, :, 0:128], 0.0), writes=[r_mix[k][0] for k in range(8)])
        for k in range(8):
            S.op("dve", lambda e, k=k: e.tensor_scalar(out=gB2[:, k, :], in0=onesb[:], scalar1=ppt[:, P_G2 + k:P_G2 + k + 1],
                                                       scalar2=None, op0=ALU.mult),
                 reads=[r_ones, r_pp], writes=[r_gB2])

        stat_res = {}

        def norm_stats(src_ap, r_src, key):
            col = norm_stats.n % 128
            norm_stats.n += 1
            rs = Res("stat%d" % norm_stats.n)
            stat_res[key] = rs
            junk = junks[norm_stats.n % 2]
            r_junk = r_junks[norm_stats.n % 2]
            S.op("act", lambda e: e.activation(out=junk[:], in_=src_ap, func=AF.Square, accum_out=stats[:, 0, col:col + 1]),
                 reads=(r_src if isinstance(r_src, list) else [r_src]), writes=[r_junk, rs])
            S.op("act", lambda e: e.activation(out=stats[:, 1, col:col + 1], in_=stats[:, 0, col:col + 1], func=AF.Ln,
                                               scale=1.0 / D, bias=epst[:, 0:1]), reads=[rs, r_eps], writes=[rs])
            S.op("act", lambda e: e.activation(out=stats[:, 2, col:col + 1], in_=stats[:, 1, col:col + 1], func=AF.Exp,
                                               scale=-0.5), reads=[rs], writes=[rs])
            return stats[:, 2, col:col + 1], rs

        norm_stats.n = 0

        def dump(name, src_ap, r_src):
            if name in dbg_out:
                S.dma("sp", lambda e: e.dma_start(out=dbg_out[name], in_=src_ap), reads=r_src, sem_owner=r_dbg)

        with ExitStack() as st:
            hT = T(st, "hT", [128, 8, TK], BF16)
            r_hT = [Res("hT%d" % t) for t in range(NT)]
            masks = T(st, "masks", [128, 3, 512], BF16); r_masks = Res("masks")
            met = T(st, "met", [128, 34], BF16); r_me = Res("me")
            gB1 = T(st, "gB1", [128, 8, 128], BF16); r_gB1 = Res("gB1")
            S.dma("pool", lambda e: e.dma_start(out=masks[:], in_=cstb[:, C_MASK:C_MASK + 1536].rearrange("p (a b) -> p a b", a=3)),
                  writes=[r_masks], sem_owner=r_masks)
            S.dma("pool", lambda e: e.dma_start(out=met[:], in_=cstb[:, C_ME:C_ME + 34]), writes=[r_me], sem_owner=r_me)
            for k in range(8):
                S.op("dve", lambda e, k=k: e.tensor_scalar(out=gB1[:, k, :], in0=onesb[:], scalar1=ppt[:, P_G1 + k:P_G1 + k + 1],
                                                           scalar2=None, op0=ALU.mult),
                     reads=[r_ones, r_pp], writes=[r_gB1])

            wpl = T(st, "wpl", [128, 8, 512], BF16); r_wpl = Res("wpl")
            wq = [T(st, "wq%d" % i, [128, 8, 384], BF16) for i in range(2)]
            r_wq = [Res("wq%d" % i) for i in range(2)]

            def load_wq(c):
                sl = c % 2
                for j in range(3):
                    S.dma("pool", lambda e, j=j, sl=sl, c=c: e.dma_start(
                        out=wq[sl][:, :, j * 128:(j + 1) * 128], in_=w_in_v[:, :, j * 512 + c * 128: j * 512 + (c + 1) * 128]),
                        writes=[r_wq[sl]], sem_owner=r_wq[sl])

            load_wq(0)
            load_wq(1)
            with ExitStack() as s1:
                NXI = 6
                xin = [T(s1, "xin%d" % i, [128, D], F32) for i in range(NXI)]
                r_xin = [Res("xin%d" % i) for i in range(NXI)]
                xs = [T(s1, "xs%d" % i, [128, D], BF16) for i in range(2)]
                r_xs = [Res("xs%d" % i) for i in range(2)]
                psT = [PS(s1, "psT%d" % i, [128, 8, 128], BF16) for i in range(2)]
                r_psT = [Res("psT%d" % i, psum=True) for i in range(2)]
                p1 = {}

                def p1_load_sq(t):
                    a = t % NXI
                    S.dma("sp", lambda e: e.dma_start(out=xin[a][:], in_=xk[t * 128:(t + 1) * 128, :]),
                          writes=[r_xin[a]], sem_owner=r_xin[a])
                    col = norm_stats.n % 128
                    norm_stats.n += 1
                    rs = Res("stat%d" % norm_stats.n)
                    junk = junks[norm_stats.n % 2]
                    r_junk = r_junks[norm_stats.n % 2]
                    S.op("act", lambda e: e.activation(out=junk[:], in_=xin[a][:], func=AF.Square, accum_out=stats[:, 0, col:col + 1]),
                         reads=[r_xin[a]], writes=[r_junk, rs])
                    p1[t] = (col, rs)

                def p1_rstd_scale(t):
                    a, b2 = t % NXI, t % 2
                    col, rs = p1[t]
                    S.op("act", lambda e: e.activation(out=stats[:, 1, col:col + 1], in_=stats[:, 0, col:col + 1], func=AF.Ln,
                                                       scale=1.0 / D, bias=epst[:, 0:1]), reads=[rs, r_eps], writes=[rs])
                    S.op("act", lambda e: e.activation(out=stats[:, 2, col:col + 1], in_=stats[:, 1, col:col + 1], func=AF.Exp,
                                                       scale=-0.5), reads=[rs], writes=[rs])
                    rstd = stats[:, 2, col:col + 1]
                    if t % 3 != 0:
                        S.op("dve", lambda e: e.tensor_scalar(out=xs[b2][:], in0=xin[a][:], scalar1=rstd, scalar2=None, op0=ALU.mult),
                             reads=[r_xin[a], rs], writes=[r_xs[b2]])
                    else:
                        S.op("act", lambda e: e.activation(out=xs[b2][:], in_=xin[a][:], func=AF.Copy, scale=rstd),
                             reads=[r_xin[a], rs], writes=[r_xs[b2]])
                    for k in range(8):
                        S.op("pe", lambda e, k=k: e.transpose(out=psT[b2][:, k, :], in_=xs[b2][:, k * 128:(k + 1) * 128], identity=ident[:]),
                             reads=[r_xs[b2], r_ident], writes=[r_psT[b2]], signal=(k == 7))

                def p1_evac(t):
                    b2 = t % 2
                    S.op("dve", lambda e: e.tensor_tensor(out=hT[:, :, t * 128:(t + 1) * 128], in0=psT[b2][:], in1=gB1[:], op=ALU.mult),
                         reads=[r_psT[b2], r_gB1], writes=[r_hT[t]])

                for i in range(NT + 2):
                    if i < NT:
                        p1_load_sq(i)
                    if 0 <= i - 2 < NT:
                        p1_evac(i - 2)
                    if 0 <= i - 1 < NT:
                        p1_rstd_scale(i - 1)
            snapA = S.snapshot()

            with ExitStack() as s2:
                qT = T(s2, "qT", [128, NQ], BF16); r_qT = Res("qT", snapA)
                kT = T(s2, "kT", [128, TK], BF16); r_kT = Res("kT", snapA)
                vT = T(s2, "vT", [128, TK], BF16); r_vT = Res("vT", snapA)
                Vc = T(s2, "Vc", [128, NB + 2, 130], BF16); r_Vc = Res("Vc", snapA)
                NPT = 8
                Pt = [T(s2, "Pt%d" % i, [128, 512], BF16) for i in range(NPT)]
                r_Pt = [Res("Pt%d" % i, snapA) for i in range(NPT)]
                acc = [T(s2, "acc%d" % i, [128, OWN + 8], F32) for i in range(2)]
                r_acc = [Res("acc%d" % i, snapA) for i in range(2)]
                rden = [T(s2, "rden%d" % i, [128, 512], F32) for i in range(2)]
                r_rden = [Res("rden%d" % i, snapA) for i in range(2)]
                Pe = [T(s2, "Pe%d" % i, [128, 34], BF16) for i in range(2)]
                r_Pe = [Res("Pe%d" % i, snapA) for i in range(2)]
                selA = cstf[0:65, C_SELA:C_SELA + 64]
                selB = cstf[:, C_SELB:C_SELB + 128]
                pj = [PS(s2, "pj%d" % i, [128, 512], F32) for i in range(2)]
                r_pj = [Res("pj%d" % i, snapA, psum=True) for i in range(2)]
                psS = [PS(s2, "psS%d" % i, [128, 512], F32) for i in range(4)]
                r_psS = [Res("psS%d" % i, snapA, psum=True) for i in range(4)]
                psO = [PS(s2, "psO%d" % i, [128, 512], F32) for i in range(2)]
                r_psO = [Res("psO%d" % i, snapA, psum=True) for i in range(2)]

                S.op("pool", lambda e: e.memset(Vc[:], 1.0), writes=[r_Vc])

                cast_jobs = [(plw_bf, pool_w, r_plwbf), (wout_bf, w_out, r_woutbf)]
                for i4 in range(4):
                    cast_jobs.append((wup_bf[256 * i4:256 * i4 + 256, :], w_up[256 * i4:256 * i4 + 256, :], r_wupbf))
                for i4 in range(2):
                    cast_jobs.append((wdn_bf[1408 * i4:1408 * i4 + 1408, :], w_down[1408 * i4:1408 * i4 + 1408, :], r_wdnbf))

                def issue_casts(n):
                    for _ in range(n):
                        if cast_jobs:
                            dst, src, rr = cast_jobs.pop(0)
                            S.dma("pool", lambda e, dst=dst, src=src: e.dma_start(out=dst, in_=src), writes=[rr], sem_owner=rr)
                pjn = [0]

                def proj_fm(wap_fn, r_w, tok0, ntok, dst_fn, r_dst, evac):
                    c0 = 0
                    while c0 < ntok:
                        n = min(512, ntok - c0)
                        sl = pjn[0] % 2
                        pjn[0] += 1
                        tiles = sorted(set(range((tok0 + c0) // 128, (tok0 + c0 + n - 1) // 128 + 1)))
                        for k in range(8):
                            S.op("pe", lambda e, k=k, sl=sl, c0=c0, n=n: e.matmul(
                                pj[sl][:, 0:n], lhsT=wap_fn(k), rhs=hT[:, k, tok0 + c0: tok0 + c0 + n],
                                start=(k == 0), stop=(k == 7)),
                                reads=[r_w] + [r_hT[t] for t in tiles], writes=[r_pj[sl]], signal=(k == 7))
                        evac(sl, c0, n)
                        c0 += n

                for c in range(4):
                    sl_w = c % 2
                    if 1 <= c and c + 1 < 4:
                        load_wq(c + 1)
                    issue_casts(3 if c == 0 else 2)
                    if c == 3:
                        S.dma("pool", lambda e: e.dma_start(out=wpl[:], in_=w_in_v[:, :, 1536:2048]), writes=[r_wpl], sem_owner=r_wpl)
                    evn = [0]

                    def evac_to(dst, r_dst, scale=None):
                        def f(sl, c0, n):
                            evn[0] += 1
                            if scale is not None:
                                S.op("act", lambda e: e.activation(out=dst[:, c0:c0 + n], in_=pj[sl][:, 0:n], func=AF.Copy, scale=scale),
                                     reads=[r_pj[sl]], writes=[r_dst])
                            elif evn[0] % 2 == 0:
                                S.op("act", lambda e: e.copy(out=dst[:, c0:c0 + n], in_=pj[sl][:, 0:n]),
                                     reads=[r_pj[sl]], writes=[r_dst])
                            else:
                                S.op("dve", lambda e: e.tensor_copy(out=dst[:, c0:c0 + n], in_=pj[sl][:, 0:n]),
                                     reads=[r_pj[sl]], writes=[r_dst])
                        return f

                    proj_fm(lambda k: wq[sl_w][:, k, 256:384], r_wq[sl_w], 0, TK, None, r_vT, evac_to(vT, r_vT))
                    proj_fm(lambda k: wq[sl_w][:, k, 128:256], r_wq[sl_w], 0, TK, None, r_kT, evac_to(kT, r_kT))
                    proj_fm(lambda k: wq[sl_w][:, k, 0:128], r_wq[sl_w], TK - NQ, NQ, None, r_qT, evac_to(qT, r_qT, scale=0.125))
                    def bfv(t_):
                        return t_.bitcast(BF16)[:].rearrange("p (a b) -> p a b", a=8)
                    vbanks = [(bfv(pj[i]), r_pj[i]) for i in range(2)] + [(bfv(psS[i]), r_psS[i]) for i in range(4)]
                    b0 = 0
                    gi = 0
                    while b0 < NB:
                        nb = min(8, NB - b0)
                        pv_, r_pv = vbanks[gi % 6]
                        for j in range(nb):
                            S.op("pe", lambda e, j=j, b0=b0, pv_=pv_: e.transpose(out=pv_[:, j, :], in_=vT[:, blocks[b0 + j]], identity=ident[:]),
                                 reads=[r_vT, r_ident], writes=[r_pv], signal=(j == nb - 1))
                        base = Vc[:, b0:b0 + nb, 0:64]
                        pa = [list(x) for x in base.ap]
                        dst = bass.AP(Vc, base.offset, [pa[0], pa[1], [66, 2], pa[2]])
                        src = pv_[:, 0:nb, :].rearrange("p n (h d) -> p n h d", h=2)
                        if gi % 2 == 0:
                            S.op("act", lambda e, dst=dst, src=src: e.copy(out=dst, in_=src), reads=[r_pv], writes=[r_Vc])
                        else:
                            S.op("dve", lambda e, dst=dst, src=src: e.tensor_copy(out=dst, in_=src), reads=[r_pv], writes=[r_Vc])
                        b0 += nb
                        gi += 1

                    HD = [dict(hs=slice(0, 64), M=65, vcols=slice(0, 65), ac=acc[0], r_ac=r_acc[0]),
                          dict(hs=slice(64, 128), M=128, vcols=slice(2, 130), ac=acc[1], r_ac=r_acc[1])]
                    groups = []
                    for g in range(4):
                        groups.append((0, [(0, 0, 4 * g + u) for u in range(4)], g))
                    for n in range(4):
                        groups.append((1, [(1, r, n) for r in range(4)], n))
                    for g in range(4):
                        groups.append((2, [(2, 4 * g + u, 0) for u in range(4)], g))
                    pairs = []
                    for gi2, (p, units, ga) in enumerate(groups):
                        pairs.append(units[0:2])
                        pairs.append(units[2:4])
                    dil = (1, 4, 16)
                    LAG = 2
                    npairs = len(pairs)
                    eb = [bidx[(2, r, -1)] for r in range(16)] + [bidx["hh"]]

                    def emit_pv(gi2, h):
                        hd = HD[h]
                        M = hd["M"]
                        ac, r_ac = hd["ac"], hd["r_ac"]
                        p, units, ga = groups[gi2]
                        for u, (pp_, r, n) in enumerate(units):
                            pi = 2 * gi2 + u // 2
                            ptn = (2 * pi + h) % NPT
                            pt, rpt = Pt[ptn], r_Pt[ptn]
                            bprev = bidx[(pp_, r, n - 1)]
                            bcur = bidx[(pp_, r, n)]
                            off = 256 * (u % 2)
                            S.op("pe", lambda e, u=u, pt=pt, bprev=bprev, off=off: e.matmul(
                                psO[h][0:M, 128 * u:128 * u + 128], lhsT=Vc[:, bprev, hd["vcols"]], rhs=pt[:, off:off + 128],
                                start=True, stop=False), reads=[r_Vc, rpt], writes=[r_psO[h]], signal=False)
                            S.op("pe", lambda e, u=u, pt=pt, bcur=bcur, off=off: e.matmul(
                                psO[h][0:M, 128 * u:128 * u + 128], lhsT=Vc[:, bcur, hd["vcols"]], rhs=pt[:, off + 128:off + 256],
                                start=False, stop=True), reads=[r_Vc, rpt], writes=[r_psO[h]], signal=(u == 3))
                        if p == 0:
                            dst = ac[0:M, 512 * ga:512 * ga + 512]
                            src = psO[h][0:M, :]
                        elif p == 1:
                            dst = ac[0:M, 512 * ga:512 * ga + 512].rearrange("p (i r) -> p r i", r=4)
                            src = psO[h][0:M, :].rearrange("p (u i) -> p u i", u=4)
                        else:
                            dst = ac[0:M, 0:OWN].rearrange("p (i r) -> p r i", r=16)[:, 4 * ga:4 * ga + 4, :]
                            src = psO[h][0:M, :].rearrange("p (u i) -> p u i", u=4)
                        if p == 0:
                            S.op("act", lambda e: e.copy(out=dst, in_=src), reads=[r_psO[h]], writes=[r_ac])
                        else:
                            S.op("dve", lambda e: e.tensor_tensor(out=dst, in0=src, in1=dst, op=ALU.add),
                                 reads=[r_psO[h], r_ac], writes=[r_ac])

                    for b, bi in enumerate(eb):
                        for h in range(2):
                            hs = HD[h]["hs"]
                            S.op("pe", lambda e, b=b, bi=bi, h=h, hs=hs: e.matmul(psS[h][:, 2 * b:2 * b + 2], lhsT=kT[hs, blocks[bi]], rhs=qT[hs, 126:128],
                                                                               start=True, stop=True),
                                 reads=[r_kT, r_qT], writes=[r_psS[h]], signal=(b == 16 and h == 1))
                    for h in range(2):
                        S.op("act", lambda e, h=h: e.activation(out=Pe[h][:], in_=psS[h][:, 0:34], func=AF.Exp), reads=[r_psS[h]], writes=[r_Pe[h]])
                        S.op("dve", lambda e, h=h: e.tensor_tensor(out=Pe[h][:], in0=Pe[h][:], in1=met[:], op=ALU.mult),
                             reads=[r_Pe[h], r_me], writes=[r_Pe[h]])

                    def e_pv(h):
                        hd = HD[h]
                        M = hd["M"]
                        for b, bi in enumerate(eb):
                            S.op("pe", lambda e, b=b, bi=bi: e.matmul(psO[h][0:M, 0:2], lhsT=Vc[:, bi, hd["vcols"]], rhs=Pe[h][:, 2 * b:2 * b + 2],
                                                                      start=(b == 0), stop=(b == 16)),
                                 reads=[r_Vc, r_Pe[h]], writes=[r_psO[h]], signal=(b == 16))
                        S.op("dve", lambda e: e.tensor_copy(out=hd["ac"][0:M, OWN:OWN + 2], in_=psO[h][0:M, 0:2]),
                             reads=[r_psO[h]], writes=[hd["r_ac"]])

                    for i in range(npairs + LAG):
                        if i < npairs:
                            units = pairs[i]
                            sb = 2 * (i % 2)
                            nmm = 0
                            for u, (pp_, r, n) in enumerate(units):
                                d = dil[pp_]
                                qs = QOFF + 128 * n * d + r
                                qsl = slice(qs, qs + 127 * d + 1, d)
                                kb = [blocks[bidx[(pp_, r, n - 1)]], blocks[bidx[(pp_, r, n)]]]
                                for wch in range(2):
                                    for h in range(2):
                                        hs = HD[h]["hs"]
                                        nmm += 1
                                        S.op("pe", lambda e, u=u, wch=wch, h=h, hs=hs, qsl=qsl, kb=kb: e.matmul(
                                            psS[sb + h][:, 256 * u + 128 * wch:256 * u + 128 * wch + 128], lhsT=kT[hs, kb[wch]], rhs=qT[hs, qsl],
                                            start=True, stop=True), reads=[r_kT, r_qT], writes=[r_psS[sb + h]], signal=(nmm == 8))
                            h0 = units[0][2] == 0
                            h1 = units[1][2] == 0
                            mk = 2 if (h0 and h1) else (1 if h0 else 0)
                            assert not (h1 and not h0)
                            for h in range(2):
                                pslot = (2 * i + h) % NPT
                                S.op("act", lambda e, h=h, pslot=pslot: e.activation(out=Pt[pslot][:], in_=psS[sb + h][:], func=AF.Exp),
                                     reads=[r_psS[sb + h]], writes=[r_Pt[pslot]])
                                meng = "pool" if (h == 0 or i % 3 == 0) else "dve"
                                S.op(meng, lambda e, pslot=pslot: e.tensor_tensor(out=Pt[pslot][:], in0=Pt[pslot][:], in1=masks[:, mk, :], op=ALU.mult),
                                     reads=[r_Pt[pslot], r_masks], writes=[r_Pt[pslot]])
                        if i == 1:
                            e_pv(0)
                            e_pv(1)
                        j = i - LAG
                        if j >= 0 and j % 2 == 1:
                            emit_pv(j // 2, 0)
                            emit_pv(j // 2, 1)

                    for h2 in range(2):
                        hs = HD[h2]["hs"]
                        ac, r_ac = HD[h2]["ac"], HD[h2]["r_ac"]
                        for j in range(5):
                            c0 = 512 * j
                            n = 512 if j < 4 else 2
                            sl = pjn[0] % 2
                            pjn[0] += 1
                            if h2 == 0:
                                S.op("pe", lambda e, sl=sl, c0=c0, n=n: e.matmul(pj[sl][0:64, 0:n], lhsT=selA, rhs=ac[0:65, c0:c0 + n],
                                                                                   start=True, stop=True),
                                     reads=[r_cstf, r_ac], writes=[r_pj[sl]])
                            else:
                                S.op("pe", lambda e, sl=sl, c0=c0, n=n: e.matmul(pj[sl][:, 0:n], lhsT=selB, rhs=ac[:, c0:c0 + n],
                                                                                   start=True, stop=True),
                                     reads=[r_cstf, r_ac], writes=[r_pj[sl]])
                            rd = rden[j % 2]
                            r_rd = r_rden[j % 2]
                            S.op("act", lambda e, sl=sl, n=n, rd=rd: e.activation(out=rd[hs, 0:n], in_=pj[sl][hs, 0:n], func=AF.Ln),
                                 reads=[r_pj[sl]], writes=[r_rd])
                            S.op("act", lambda e, n=n, rd=rd: e.activation(out=rd[hs, 0:n], in_=rd[hs, 0:n], func=AF.Exp, scale=-1.0),
                                 reads=[r_rd], writes=[r_rd])
                            dcol = (QOFF + c0) if j < 4 else 126
                            S.op("pool", lambda e, c0=c0, n=n, rd=rd, dcol=dcol: e.tensor_tensor(
                                out=mixinT[hs, c, dcol:dcol + n], in0=ac[hs, c0:c0 + n], in1=rd[hs, 0:n], op=ALU.mult),
                                reads=[r_ac, r_rd], writes=[r_mix[c][j + 1 if j < 4 else 0]])

            snapB = S.snapshot()
            with ExitStack() as s3:
                plw = T(s3, "plw", [128, 4, 128], BF16); r_plw = Res("plw", snapB)
                pin2 = [T(s3, "pin%d" % i, [128, NQ], F32) for i in range(2)]
                r_pin2 = [Res("pin%d" % i, snapB) for i in range(2)]
                tA2 = [T(s3, "tA%d" % i, [128, NQ], F32) for i in range(2)]
                r_tA2 = [Res("tA%d" % i, snapB) for i in range(2)]
                tB2 = [T(s3, "tB%d" % i, [128, NQ], F32) for i in range(2)]
                r_tB2 = [Res("tB%d" % i, snapB) for i in range(2)]
                t162 = [T(s3, "t16_%d" % i, [128, 16], F32) for i in range(2)]
                r_t162 = [Res("t16_%d" % i, snapB) for i in range(2)]
                pld2 = [T(s3, "pld%d" % i, [128, NQ], BF16) for i in range(2)]
                r_pld2 = [Res("pld%d" % i, snapB) for i in range(2)]
                pj = [PS(s3, "pjp%d" % i, [128, 512], F32) for i in range(3)]
                r_pj = [Res("pjp%d" % i, snapB, psum=True) for i in range(3)]
                po = [PS(s3, "pop%d" % i, [128, 512], F32) for i in range(2)]
                r_po = [Res("pop%d" % i, snapB, psum=True) for i in range(2)]
                S.dma("sp", lambda e: e.dma_start(out=plw[:], in_=plw_bf_v), reads=[r_plwbf], writes=[r_plw], sem_owner=r_plw)
                tok0 = TK - NQ
                pjn = [0]

                order = [2, 3, 1, 0]
                slot_of = {g_: i_ % 2 for i_, g_ in enumerate(order)}

                def pool_P(g):
                    pin, r_pin = pin2[slot_of[g]], r_pin2[slot_of[g]]
                    c0 = 0
                    while c0 < NQ:
                        n = min(512, NQ - c0)
                        sl = pjn[0] % 3
                        pjn[0] += 1
                        tiles = sorted(set(range((tok0 + c0) // 128, (tok0 + c0 + n - 1) // 128 + 1)))
                        for k in range(8):
                            S.op("pe", lambda e, k=k, sl=sl, c0=c0, n=n, g=g: e.matmul(
                                pj[sl][:, 0:n], lhsT=wpl[:, k, 128 * g:128 * g + 128], rhs=hT[:, k, tok0 + c0: tok0 + c0 + n],
                                start=(k == 0), stop=(k == 7)),
                                reads=[r_wpl] + [r_hT[t] for t in tiles], writes=[r_pj[sl]], signal=(k == 7))
                        S.op("act", lambda e, sl=sl, c0=c0, n=n: e.copy(out=pin[:, c0:c0 + n], in_=pj[sl][:, 0:n]),
                             reads=[r_pj[sl]], writes=[r_pin])
                        c0 += n

                def pool_E(g):
                    w = 2 << g
                    pin, r_pin = pin2[slot_of[g]], r_pin2[slot_of[g]]
                    pld, r_pld = pld2[slot_of[g]], r_pld2[slot_of[g]]
                    t16, r_t16 = t162[slot_of[g]], r_t162[slot_of[g]]
                    src, r_src = pin, r_pin
                    bufs = [(tA2[slot_of[g]], r_tA2[slot_of[g]]), (tB2[slot_of[g]], r_tB2[slot_of[g]])]
                    step = 1
                    lvl = 0
                    while step < w:
                        dst, r_dst = bufs[lvl % 2]
                        lo = 16 * (lvl + 1)
                        eng = "pool" if (lvl + g) % 2 == 0 else "dve"
                        S.op(eng, lambda e, dst=dst, src=src, lo=lo, step=step: e.tensor_tensor(
                            out=dst[:, lo:NQ], in0=src[:, lo:NQ], in1=src[:, lo - step:NQ - step], op=ALU.add),
                            reads=[r_src], writes=[r_dst])
                        src, r_src = dst, r_dst
                        step *= 2
                        lvl += 1
                    S.op("dve", lambda e, src=src, w=w: e.scalar_tensor_tensor(out=pld[:, 126:NQ], in0=src[:, 126:NQ], scalar=1.0 / w,
                                                                             in1=pin[:, 126:NQ], op0=ALU.mult, op1=ALU.subtract),
                         reads=[r_src, r_pin], writes=[r_pld])
                    S.op("pool", lambda e, src=src, g=g: e.tensor_tensor(out=t16[:], in0=src[:, 128:144],
                                                                        in1=cstf[:, C_INVC + 16 * g:C_INVC + 16 * g + 16], op=ALU.mult),
                         reads=[r_src, r_cstf], writes=[r_t16])
                    S.op("pool", lambda e: e.tensor_tensor(out=pld[:, 128:144], in0=t16[:], in1=pin[:, 128:144], op=ALU.subtract),
                         reads=[r_t16, r_pin, r_pld], writes=[r_pld])

                def pool_M(g):
                    pld, r_pld = pld2[slot_of[g]], r_pld2[slot_of[g]]
                    for j in range(5):
                        if j < 4:
                            c0, n = QOFF + 512 * j, 512
                        else:
                            c0, n = 126, 2
                        sl = j % 2
                        S.op("pe", lambda e, sl=sl, c0=c0, n=n, g=g: e.matmul(po[sl][:, 0:n], lhsT=plw[:, g, :], rhs=pld[:, c0:c0 + n],
                                                                              start=True, stop=True),
                             reads=[r_plw, r_pld], writes=[r_po[sl]])
                        S.op("act", lambda e, sl=sl, c0=c0, n=n, g=g: e.activation(out=mixinT[:, 4 + g, c0:c0 + n], in_=po[sl][:, 0:n],
                                                                                   func=AF.Copy, scale=ppt[:, P_PS + g:P_PS + g + 1]),
                             reads=[r_po[sl], r_pp], writes=[r_mix[4 + g][j + 1 if j < 4 else 0]])

                for step_ in range(5):
                    if step_ < 4:
                        pool_P(order[step_])
                        pool_E(order[step_])
                    if step_ >= 1:
                        pool_M(order[step_ - 1])
        snapC = S.snapshot()
        if "mixinT" in dbg_out:
            dump("mixinT", mixinT[:].rearrange("p a b -> p (a b)"), [r for rr in r_mix for r in rr])

        with ExitStack() as st:
            gpostB = T(st, "gpostB", [128, D], F32); r_gpost = Res("gpost", snapC)
            gfpostB = T(st, "gfpostB", [128, D], F32); r_gfpost = Res("gfpost", snapC)
            wo = T(st, "wo", [128, 8, D], BF16); r_wo = Res("wo", snapC)
            wd = T(st, "wd", [128, NFC, D], BF16); r_wd = Res("wd", snapC)
            NWU = 3
            wu = [T(st, "wu%d" % i, [128, 8, 2, 256], BF16) for i in range(NWU)]
            r_wu = [Res("wu%d" % i, snapC) for i in range(NWU)]
            x1 = T(st, "x1", [128, 4, D], F32)
            r_x1 = [Res("x1_%d" % i, snapC) for i in range(4)]
            xr = [T(st, "xr%d" % i, [128, D], F32) for i in range(2)]
            r_xr = [Res("xr%d" % i, snapC) for i in range(2)]
            r_xo = [Res("xo%d" % i) for i in range(2)]
            xs2 = [T(st, "xs2_%d" % i, [128, D], BF16) for i in range(2)]
            r_xs2 = [Res("xs2_%d" % i, snapC) for i in range(2)]
            h2T = T(st, "h2T", [128, 8, 512], BF16)
            r_h2 = [Res("h2_%d" % i, snapC) for i in range(4)]
            h2e = T(st, "h2e", [128, 8, 2], BF16); r_h2e = Res("h2e", snapC)
            yT = T(st, "yT", [128, NFC, 512], BF16)
            r_yT = [Res("yT%d" % i, snapC) for i in range(NFC)]
            x1e = yT.bitcast(F32)[:, 0:4, :].rearrange("p a b -> p (a b)")
            r_x1e = r_yT[0:4]
            cs = [[T(st, "cs%d_%d" % (p_, i), [128, 512], F32) for i in range(2)] for p_ in range(2)]
            r_cs = [[Res("cs%d_%d" % (p_, i), snapC) for i in range(2)] for p_ in range(2)]
            gl = [T(st, "gl%d" % i, [128, 512], BF16) for i in range(2)]
            r_gl = [Res("gl%d" % i, snapC) for i in range(2)]
            uprev = T(st, "uprev", [128, 2 * NFC, 2], F32)
            r_up = [Res("uprev%d" % i, snapC) for i in range(2 * NFC)]
            corr = T(st, "corr", [128, 2 * NFC, 2], F32)
            r_corr = [Res("corr%d" % i, snapC) for i in range(2 * NFC)]
            tb = T(st, "tb", [128, 4], F32)
            r_tb = [Res("tb%d" % i, snapC) for i in range(4)]
            tbn = [0]

            def make_corr(ci):
                j = tbn[0] % 4
                tbn[0] += 1
                S.op("pool", lambda e: e.tensor_tensor(out=corr[:, ci, :], in0=uprev[:, ci, :],
                                                       in1=ppt[:, P_CW2 + 3 * ci:P_CW2 + 3 * ci + 2], op=ALU.mult),
                     reads=[r_up[ci], r_pp], writes=[r_corr[ci]])
                S.op("pool", lambda e: e.tensor_tensor(out=tb[:, j:j + 1], in0=uprev[:, ci, 1:2],
                                                       in1=ppt[:, P_CW2 + 3 * ci + 2:P_CW2 + 3 * ci + 3], op=ALU.mult),
                     reads=[r_up[ci], r_pp], writes=[r_tb[j]])
                S.op("pool", lambda e: e.tensor_tensor(out=corr[:, ci, 0:1], in0=corr[:, ci, 0:1], in1=tb[:, j:j + 1], op=ALU.add),
                     reads=[r_corr[ci], r_tb[j]], writes=[r_corr[ci]])
            pm = [PS(st, "pm%d" % i, [128, D], F32) for i in range(2)]
            r_pmh = [[Res("pm%d_%d" % (i, h), snapC, psum=True) for h in range(2)] for i in range(2)]
            r_pm = [r_pmh[0], r_pmh[1]]
            pu_t = [[PS(st, "pu%d_%d" % (p_, i), [128, 512], F32) for i in range(2)] for p_ in range(2)]
            r_pu_t = [[Res("pu%d_%d" % (p_, i), snapC, psum=True) for i in range(2)] for p_ in range(2)]
            pu_slots = [[(pu_t[p_][0][:], r_pu_t[p_][0]), (pu_t[p_][1][:], r_pu_t[p_][1]),
                         (pm[1][:, 512 * p_:512 * p_ + 512], r_pmh[1][p_]), (pm[0][:, 512 * p_:512 * p_ + 512], r_pmh[0][p_])]
                        for p_ in range(2)]
            ptr = pu_t[0][0].bitcast(BF16)[:].rearrange("p (a b) -> p a b", a=8)
            r_ptr = r_pu_t[0][0]
            pue = pm[0]

            S.dma("sp", lambda e: e.dma_start(out=wo[:], in_=wout_bf_v), reads=[r_woutbf], writes=[r_wo], sem_owner=r_wo)
            S.dma("sp", lambda e: e.dma_start(out=gpostB[:], in_=g_post.partition_broadcast(128)),
                  writes=[r_gpost], sem_owner=r_gpost)
            NG = NFC // 2
            wu_loaded = [0]

            def load_wu_next():
                n = wu_loaded[0]
                if n >= 4 * NG:
                    return
                wu_loaded[0] += 1
                g2 = n % NG
                sl = n % NWU
                for part in range(2):
                    col = part * DFF + 256 * g2
                    S.dma("sp", lambda e, sl=sl, part=part, col=col: e.dma_start(out=wu[sl][:, :, part, :], in_=wup_bf_v[:, :, col:col + 256]),
                          reads=[r_wupbf], writes=[r_wu[sl]], sem_owner=r_wu[sl])

            tcnt = [0]

            def stage_m(tiles):
                n = len(tiles)
                st_ = [dict() for _ in range(n)]

                def S1(j):
                    mcol, xrow, x1_ap, r_x1t, h2_fn = tiles[j]
                    i = tcnt[0]
                    tcnt[0] += 1
                    sl = i % 2
                    st_[j]["sl"] = sl
                    S.dma("sp", lambda e: e.dma_start(out=xr[sl][:], in_=xk[xrow:xrow + 128, :]), writes=[r_xr[sl]], sem_owner=r_xr[sl])
                    jblk = 0 if mcol < QOFF else 1 + (mcol - QOFF) // 512
                    for half in range(2):
                        for k in range(8):
                            S.op("pe", lambda e, half=half, k=k: e.matmul(pm[sl][:, 512 * half:512 * half + 512], lhsT=mixinT[:, k, mcol:mcol + 128],
                                                                          rhs=wo[:, k, 512 * half:512 * half + 512], start=(k == 0), stop=(k == 7)),
                                 reads=[r_mix[k][jblk], r_wo], writes=r_pm[sl], signal=(half == 1 and k == 7))
                    rstd, rs = norm_stats(pm[sl][:], r_pm[sl], ("ma", mcol))
                    S.op("dve", lambda e: e.scalar_tensor_tensor(out=x1_ap, in0=pm[sl][:], scalar=rstd, in1=gpostB[:], op0=ALU.mult, op1=ALU.mult),
                         reads=r_pm[sl] + [rs, r_gpost], writes=r_x1t)
                    S.op("dve", lambda e: e.tensor_tensor(out=x1_ap, in0=x1_ap, in1=xr[sl][:], op=ALU.add),
                         reads=r_x1t + [r_xr[sl]], writes=r_x1t)

                def S2(j):
                    mcol, xrow, x1_ap, r_x1t, h2_fn = tiles[j]
                    sl = st_[j]["sl"]
                    rstd2, rs2 = norm_stats(x1_ap, r_x1t, ("mb", mcol))
                    S.op("dve", lambda e: e.tensor_scalar(out=xs2[sl][:], in0=x1_ap, scalar1=rstd2, scalar2=None, op0=ALU.mult),
                         reads=r_x1t + [rs2], writes=[r_xs2[sl]])
                    for k in range(8):
                        S.op("pe", lambda e, k=k: e.transpose(out=ptr[:, k, :], in_=xs2[sl][:, 128 * k:128 * k + 128], identity=ident[:]),
                             reads=[r_xs2[sl], r_ident], writes=[r_ptr], signal=(k == 7))

                def S3(j):
                    tiles[j][4]()

                for step in range(n + 3):
                    if 0 <= step - 3 < n:
                        S3(step - 3)
                    if step < n:
                        S1(step)
                    if 0 <= step - 2 < n:
                        S2(step - 2)

            def h2_evac_e():
                S.op("dve", lambda e: e.tensor_tensor(out=h2e[:], in0=ptr[:, :, 126:128], in1=gB2[:, :, 126:128], op=ALU.mult),
                     reads=[r_ptr, r_gB2], writes=[r_h2e])

            def mk_h2_evac(t):
                def f():
                    S.op("dve", lambda e: e.tensor_tensor(out=h2T[:, :, 128 * t:128 * t + 128], in0=ptr, in1=gB2[:], op=ALU.mult),
                         reads=[r_ptr, r_gB2], writes=[r_h2[t]])
                return f

            gidx = [0]
            for b in range(4):
                tiles = []
                if b == 0:
                    tiles.append((0, 2048, x1e, r_x1e, h2_evac_e))
                for t in range(4):
                    tiles.append((QOFF + 512 * b + 128 * t, OWN0 + 512 * b + 128 * t, x1[:, t, :], [r_x1[t]], mk_h2_evac(t)))
                stage_m(tiles)
                if b == 0:
                    load_wu_next()
                    load_wu_next()
                    S.dma("sp", lambda e: e.dma_start(out=gfpostB[:], in_=g_fpost.partition_broadcast(128)),
                          writes=[r_gfpost], sem_owner=r_gfpost)
                    for k2 in range(2):
                        S.dma("sp", lambda e, k2=k2: e.dma_start(out=wd[:, 11 * k2:11 * k2 + 11, :], in_=wdn_bf_v[:, 11 * k2:11 * k2 + 11, :]),
                              reads=[r_wdnbf], writes=[r_wd], sem_owner=r_wd)
                    dump("x1", x1[:].rearrange("p a b -> p (a b)"), r_x1)
                for i in range(NFC + 1):
                    if i < NFC:
                        fc = i
                        f2 = fc % 2
                        sl2 = fc % 2
                        nsl = 3 if b == 0 else 4
                        pus = [pu_slots[part][fc % nsl] for part in range(2)]
                        if f2 == 0:
                            slw = gidx[0] % NWU
                            gidx[0] += 1
                            load_wu_next()
                        cis = [part * NFC + fc for part in range(2)]
                        for part in range(2):
                            ps_, r_ps = pus[part]
                            for k in range(8):
                                S.op("pe", lambda e, k=k, part=part, f2=f2, ps_=ps_, slw=slw: e.matmul(
                                    ps_, lhsT=wu[slw][:, k, part, 128 * f2:128 * f2 + 128], rhs=h2T[:, k, :],
                                    start=(k == 0), stop=(k == 7)),
                                    reads=[r_wu[slw]] + r_h2, writes=[r_ps], signal=(k == 7))
                            if b == 0:
                                ci = cis[part]
                                for k in range(8):
                                    S.op("pe", lambda e, k=k, part=part, f2=f2, ci=ci, slw=slw: e.matmul(
                                        pue[:, 512 * part + 2 * fc:512 * part + 2 * fc + 2], lhsT=wu[slw][:, k, part, 128 * f2:128 * f2 + 128], rhs=h2e[:, k, :],
                                        start=(k == 0), stop=(k == 7)),
                                        reads=[r_wu[slw], r_h2e], writes=[r_pmh[0][part]], signal=(k == 7))
                                S.op("act", lambda e, ci=ci, part=part: e.copy(out=uprev[:, ci, :], in_=pue[:, 512 * part + 2 * fc:512 * part + 2 * fc + 2]),
                                     reads=[r_pmh[0][part]], writes=[r_up[ci]])
                                make_corr(ci)
                        cwf = lambda ci, j: ppt[:, P_CW + 3 * ci + j:P_CW + 3 * ci + j + 1]
                        for part in range(2):
                            ci = cis[part]
                            S.op("act", lambda e, part=part, ci=ci: e.activation(
                                out=cs[part][sl2][:], in_=pus[part][0], func=AF.Identity, scale=cwf(ci, 2), bias=ppt[:, P_CB + ci:P_CB + ci + 1]),
                                reads=[pus[part][1], r_pp], writes=[r_cs[part][sl2]])
                            if b < 3:
                                S.op("act", lambda e, part=part, ci=ci: e.copy(out=uprev[:, ci, :], in_=pus[part][0][:, 510:512]),
                                     reads=[pus[part][1]], writes=[r_up[ci]])
                        for part in range(2):
                            ci = cis[part]
                            S.op("dve", lambda e, part=part, ci=ci: e.scalar_tensor_tensor(
                                out=cs[part][sl2][:, 1:512], in0=pus[part][0][:, 0:511], scalar=cwf(ci, 1), in1=cs[part][sl2][:, 1:512],
                                op0=ALU.mult, op1=ALU.add),
                                reads=[pus[part][1], r_pp, r_cs[part][sl2]], writes=[r_cs[part][sl2]])
                        for part in range(2):
                            ci = cis[part]
                            S.op("dve", lambda e, part=part, ci=ci: e.scalar_tensor_tensor(
                                out=cs[part][sl2][:, 2:512], in0=pus[part][0][:, 0:510], scalar=cwf(ci, 0), in1=cs[part][sl2][:, 2:512],
                                op0=ALU.mult, op1=ALU.add),
                                reads=[pus[part][1], r_pp, r_cs[part][sl2]], writes=[r_cs[part][sl2]])
                    if i >= 1:
                        fc = i - 1
                        sl2 = fc % 2
                        for part in range(2):
                            ci = part * NFC + fc
                            S.op("pool", lambda e, part=part, ci=ci, sl2=sl2: e.tensor_tensor(
                                out=cs[part][sl2][:, 0:2], in0=cs[part][sl2][:, 0:2], in1=corr[:, ci, :], op=ALU.add),
                                reads=[r_cs[part][sl2], r_corr[ci]], writes=[r_cs[part][sl2]])
                        S.op("act", lambda e, sl2=sl2: e.activation(out=gl[sl2][:], in_=cs[0][sl2][:], func=AF.Gelu_apprx_tanh),
                             reads=[r_cs[0][sl2]], writes=[r_gl[sl2]])
                        S.op("pool", lambda e, fc=fc, sl2=sl2: e.tensor_tensor(out=yT[:, fc, :], in0=gl[sl2][:], in1=cs[1][sl2][:], op=ALU.mult),
                             reads=[r_gl[sl2], r_cs[1][sl2]], writes=[r_yT[fc]])
                        if b < 3:
                            for part in range(2):
                                make_corr(part * NFC + fc)
                for t in range(4):
                    sl = tcnt[0] % 2
                    tcnt[0] += 1
                    extra = []
                    for half in range(2):
                        for fc in range(NFC):
                            S.op("pe", lambda e, half=half, fc=fc, t=t: e.matmul(
                                pm[sl][:, 512 * half:512 * half + 512], lhsT=yT[:, fc, 128 * t:128 * t + 128],
                                rhs=wd[:, fc, 512 * half:512 * half + 512], start=(fc == 0), stop=(fc == NFC - 1)),
                                reads=[r_yT[fc], r_wd], writes=r_pm[sl], signal=(half == 1 and fc == NFC - 1))
                    rstd3, rs3 = norm_stats(pm[sl][:], r_pm[sl], ("f", b, t))
                    S.op("dve", lambda e, t=t: e.scalar_tensor_tensor(out=xr[sl][:], in0=pm[sl][:], scalar=rstd3, in1=gfpostB[:],
                                                                    op0=ALU.mult, op1=ALU.mult),
                         reads=r_pm[sl] + [rs3, r_gfpost], writes=[r_xr[sl]])
                    S.op("pool", lambda e, t=t: e.tensor_tensor(out=xr[sl][:], in0=xr[sl][:], in1=x1[:, t, :], op=ALU.add),
                         reads=[r_xr[sl], r_x1[t]], writes=[r_xr[sl]])
                    row = 512 * b + 128 * t
                    S.dma("sp", lambda e, row=row: e.dma_start(out=out[row:row + 128, :], in_=xr[sl][:]),
                          reads=[r_xr[sl]], sem_owner=r_xo[sl])

            toks = [(r_xo[0].dsem, r_xo[0].dcnt), (r_xo[1].dsem, r_xo[1].dcnt)]
            if r_dbg.dsem is not None:
                toks.append((r_dbg.dsem, r_dbg.dcnt))
            S.wait_tokens("sp", toks)
        for E in S.ENG:
            assert not S.pending[E], E
    return nc


def _host_consts(j):
    q0 = j * OWN
    c = np.zeros((128, CST_COLS), np.float32)
    cb = np.zeros((128, CSTB_COLS), np.float32)
    cb[:, C_ID:C_ID + 128] = np.eye(128, dtype=np.float32)
    ik = np.arange(128)[:, None]
    iq = np.arange(128)[None, :]
    mprev = (ik >= iq).astype(np.float32)
    mcur = (ik <= iq).astype(np.float32)
    mhalo = mprev if j > 0 else np.zeros_like(mprev)
    kinds = [(mprev, mprev), (mhalo, mprev), (mhalo, mhalo)]
    for m, (pa, pb) in enumerate(kinds):
        cb[:, C_MASK + 512 * m:C_MASK + 512 * (m + 1)] = np.concatenate([pa, mcur, pb, mcur], axis=1)
    me = np.zeros((128, 17, 2), np.float32)
    for b in range(17):
        i = np.arange(128)
        if b < 16:
            pk = q0 - 2048 + 16 * i + b
        else:
            pk = q0 - 2176 + i
        for e in range(2):
            pq = q0 - 2 + e
            diff = pq - pk
            cnt = np.zeros(128, np.float32)
            for d in (1, 4, 16):
                cnt += ((diff >= 0) & (diff % d == 0) & (diff // d <= 128)).astype(np.float32)
            if j > 0:
                cnt *= (pk >= 0)
            me[:, b, e] = cnt
    cb[:, C_ME:C_ME + 34] = me.reshape(128, 34)
    c[64, C_SELA:C_SELA + 64] = 1.0
    c[62, C_SELB + 64:C_SELB + 128] = 1.0
    for g in range(4):
        w = 2 << g
        pos = q0 + np.arange(16)
        c[:, C_INVC + 16 * g:C_INVC + 16 * g + 16] = (1.0 / np.minimum(pos + 1, w))[None, :]
    return c, cb


_NC_CACHE = {}


def kernel(x, g_mix_pre, w_in, pool_w, pool_scale, w_out, g_mix_post, g_ffn_pre, w_up, conv_w, conv_b, w_down,
           g_ffn_post, _dbg=None):
    f = lambda a: np.ascontiguousarray(np.asarray(a, dtype=np.float32))
    x = f(x)
    B = x.shape[0]
    pp = np.zeros((128, PP_COLS), np.float32)
    pp[:, P_G1:P_G1 + 8] = f(g_mix_pre)[0].reshape(8, 128).T
    pp[:, P_G2:P_G2 + 8] = f(g_ffn_pre)[0].reshape(8, 128).T
    pp[:, P_PS:P_PS + 4] = f(pool_scale)[0].reshape(4, 128).T
    cw = f(conv_w)[0].reshape(3, 44, 128)
    pp[:, P_CW:P_CW + 132] = cw.transpose(2, 1, 0).reshape(128, 132)
    pp[:, P_CB:P_CB + 44] = f(conv_b)[0].reshape(44, 128).T
    cw2 = np.stack([cw[0], cw[0], cw[1]], axis=0)
    pp[:, P_CW2:P_CW2 + 132] = cw2.transpose(2, 1, 0).reshape(128, 132)
    shared = {
        "pp": pp,
        "w_in": f(w_in)[0], "pool_w": f(pool_w)[0].reshape(512, 128), "w_out": f(w_out)[0],
        "w_up": f(w_up)[0], "w_down": f(w_down)[0],
        "g_mix_post": f(g_mix_post)[0], "g_ffn_post": f(g_ffn_post)[0],
    }
    in_maps = []
    for c in range(NCORES):
        b, j = divmod(c, 4)
        q0 = j * OWN
        xkc = np.zeros((TK, D), np.float32)
        lo = q0 - OWN0
        s = max(lo, 0)
        xkc[s - lo:, :] = x[b, s:q0 + OWN, :]
        m = dict(shared)
        m["xk"] = xkc
        m["cst"], m["cstb"] = _host_consts(j)
        in_maps.append(m)
    key = repr(_dbg)
    if key not in _NC_CACHE:
        _NC_CACHE[key] = build_nc(_dbg)
    nc = _NC_CACHE[key]
    res = run_bass_kernel_spmd(nc, in_maps, core_ids=list(range(NCORES)))
    outs = [np.asarray(r["out"], dtype=np.float32) for r in res.results]
    full = np.stack(outs).reshape(B, SEQ, D)
    if _dbg:
        return full, res.results
    return full
```

```python
import numpy as np
from contextlib import ExitStack
import concourse.bass as bass
import concourse.mybir as mybir
from concourse.bass_utils import run_bass_kernel_spmd

F32 = mybir.dt.float32
BF16 = mybir.dt.bfloat16
AF = mybir.ActivationFunctionType
ALU = mybir.AluOpType

D = 1024
SEQ = 8192
NCORES = 8
OWN = 2048
TK = 4224
NT = TK // 128
OWN0 = TK - OWN
NQ = 2176
QOFF = 128
DFF = 2816
NFC = 22
EPS = 1e-6

C_ID = 0
C_MASK = C_ID + 128
C_ME = C_MASK + 1536
CSTB_COLS = C_ME + 34
C_SELA = 0
C_SELB = C_SELA + 64
C_INVC = C_SELB + 128
CST_COLS = C_INVC + 64
P_G1 = 0
P_G2 = 8
P_PS = 16
P_CW = 20
P_CB = P_CW + 132
P_CW2 = P_CB + 44
PP_COLS = P_CW2 + 132


class Res:
    __slots__ = ("name", "w", "r", "dsem", "dcnt", "psum")

    def __init__(self, name, after=None, psum=False):
        self.name = name
        self.psum = psum
        self.w = None
        self.r = dict(after) if after else {}
        self.dsem = None
        self.dcnt = 0


class Sched:
    ENG = ("pe", "act", "dve", "pool", "sp")
    HANDLES = {"pe": "tensor", "act": "scalar", "dve": "vector", "pool": "gpsimd", "sp": "sync"}

    def __init__(self, nc, stack):
        self.nc = nc
        self.stack = stack
        self.sems = {}
        self.cnt = {}
        self.waited = {k: {} for k in self.ENG}
        self.pending = {k: False for k in self.ENG}
        self.dlatest = {}
        for k in self.ENG:
            self.sems[k] = stack.enter_context(nc.semaphore("s_" + k))
            self.cnt[k] = 0
        self.nd = 0
        self.ninst = 0

    def new_dsem(self):
        self.nd += 1
        key = "d%d" % self.nd
        self.sems[key] = self.stack.enter_context(self.nc.semaphore("s_" + key))
        return key

    def snapshot(self):
        snap = {}
        for k in self.ENG:
            v = self.cnt[k] + (1 if self.pending[k] else 0)
            if v:
                snap[k] = v
        snap.update(self.dlatest)
        return snap

    def _deps(self, E, reads, writes):
        best = {}
        for r in reads:
            if r.w is not None:
                k, v = r.w
                if not (k == E and E == "pe"):
                    best[k] = max(best.get(k, 0), v)
            if r.psum:
                for k, v in r.r.items():
                    if k != E:
                        best[k] = max(best.get(k, 0), v)
        for w in writes:
            if w.w is not None:
                k, v = w.w
                if not (k == E and E == "pe"):
                    best[k] = max(best.get(k, 0), v)
            for k, v in w.r.items():
                if not (k == E and E == "pe"):
                    best[k] = max(best.get(k, 0), v)
        out = []
        for k, v in best.items():
            if self.waited[E].get(k, 0) < v:
                self.waited[E][k] = v
                out.append((k, v))
        return out

    def _emit(self, E, waits, fn, inc):
        eng = getattr(self.nc, self.HANDLES[E])
        for k, v in waits:
            eng.wait_ge(self.sems[k], v)
        if fn is not None:
            ins = fn(eng)
            self.ninst += 1
            if inc is not None:
                ins.then_inc(self.sems[inc[0]], inc[1])

    def op(self, E, fn, reads=(), writes=(), signal=True):
        waits = self._deps(E, reads, writes)
        if signal:
            self.cnt[E] += 1
            tok = (E, self.cnt[E])
            self.pending[E] = False
        else:
            tok = (E, self.cnt[E] + 1)
            self.pending[E] = True
        self._emit(E, waits, fn, (E, 1) if signal else None)
        for r in reads:
            r.r[tok[0]] = max(r.r.get(tok[0], 0), tok[1])
        for w in writes:
            w.w = tok
            w.r = {}
        return tok

    def dma(self, E, fn, reads=(), writes=(), sem_owner=None):
        waits = self._deps(E, reads, writes)
        own = sem_owner
        if own.dsem is None:
            own.dsem = self.new_dsem()
        own.dcnt += 16
        tok = (own.dsem, own.dcnt)
        self.dlatest[own.dsem] = own.dcnt
        self._emit(E, waits, fn, (own.dsem, 16))
        for r in reads:
            r.r[tok[0]] = max(r.r.get(tok[0], 0), tok[1])
        for w in writes:
            w.w = tok
            w.r = {}
        return tok

    def wait_tokens(self, E, toks):
        waits = []
        for k, v in toks:
            if self.waited[E].get(k, 0) < v:
                self.waited[E][k] = v
                waits.append((k, v))
        if waits:
            self._emit(E, waits, None, None)


def _blocks():
    blk = []
    idx = {}
    for p, d in ((0, 1), (1, 4), (2, 16)):
        for r in range(d):
            for m in range(-1, 16 // d):
                s = OWN0 + 128 * d * m + r
                idx[(p, r, m)] = len(blk)
                blk.append(slice(s, s + 127 * d + 1, d))
    idx["hh"] = len(blk)
    blk.append(slice(0, 128, 1))
    return blk, idx


def build_nc(dbg=None):
    nc = bass.Bass("TRN2", target_bir_lowering=False)

    def din(name, shape, dt=F32):
        return nc.dram_tensor(name, shape, dt, kind="ExternalInput").ap()

    xk = din("xk", [TK, D])
    cst = din("cst", [128, CST_COLS])
    cstb = din("cstb", [128, CSTB_COLS])
    pp = din("pp", [128, PP_COLS])
    w_in = din("w_in", [D, 2048])
    pool_w = din("pool_w", [512, 128])
    w_out = din("w_out", [D, D])
    w_up = din("w_up", [D, 2 * DFF])
    w_down = din("w_down", [DFF, D])
    g_post = din("g_mix_post", [D])
    g_fpost = din("g_ffn_post", [D])
    out = nc.dram_tensor("out", [OWN, D], F32, kind="ExternalOutput").ap()
    dbg_out = {}
    if dbg:
        for name, (shape, dt) in dbg.items():
            dbg_out[name] = nc.dram_tensor("dbg_" + name, shape, dt, kind="ExternalOutput").ap()

    wup_bf = nc.dram_tensor("wup_bf", [D, 2 * DFF], BF16, kind="Internal").ap()
    wup_bf_v = wup_bf.rearrange("(k p) n -> p k n", p=128)
    wout_bf = nc.dram_tensor("wout_bf", [D, D], BF16, kind="Internal").ap()
    wout_bf_v = wout_bf.rearrange("(k p) n -> p k n", p=128)
    wdn_bf = nc.dram_tensor("wdn_bf", [DFF, D], BF16, kind="Internal").ap()
    wdn_bf_v = wdn_bf.rearrange("(k p) n -> p k n", p=128)
    plw_bf = nc.dram_tensor("plw_bf", [512, 128], BF16, kind="Internal").ap()
    plw_bf_v = plw_bf.rearrange("(g p) n -> p g n", p=128)
    w_in_v = w_in.rearrange("(k p) n -> p k n", p=128)
    w_out_v = w_out.rearrange("(k p) n -> p k n", p=128)
    w_up_v = w_up.rearrange("(k p) n -> p k n", p=128)
    w_down_v = w_down.rearrange("(k p) n -> p k n", p=128)
    pool_w_v = pool_w.rearrange("(g p) n -> p g n", p=128)

    blocks, bidx = _blocks()
    NB = len(blocks)

    with ExitStack() as gst:
        S = Sched(nc, gst)

        def T(st, name, shape, dt):
            return st.enter_context(nc.sbuf_tensor(name, shape, dt))

        def PS(st, name, shape, dt):
            return st.enter_context(nc.psum_tensor(name, shape, dt))

        r_out = Res("out")
        r_dbg = Res("dbg")

        cstf = T(gst, "cstf", [128, CST_COLS], F32); r_cstf = Res("cstf")
        ppt = T(gst, "ppt", [128, PP_COLS], F32); r_pp = Res("pp")
        ident = T(gst, "ident", [128, 128], BF16); r_ident = Res("ident")
        gB2 = T(gst, "gB2", [128, 8, 128], BF16); r_gB2 = Res("gB2")
        onesb = T(gst, "onesb", [128, 128], BF16); r_ones = Res("ones")
        epst = T(gst, "epst", [128, 1], F32); r_eps = Res("eps")
        stats = T(gst, "stats", [128, 3, 128], F32)
        junks = [T(gst, "junk%d" % i, [128, D], BF16) for i in range(2)]
        r_junks = [Res("junk%d" % i) for i in range(2)]
        mixinT = T(gst, "mixinT", [128, 8, NQ], BF16)
        r_mix = [[Res("mix%d_%d" % (k, j)) for j in range(5)] for k in range(8)]

        r_wupbf = Res("wupbf")
        r_woutbf = Res("woutbf")
        r_wdnbf = Res("wdnbf")
        r_plwbf = Res("plwbf")
        S.dma("sp", lambda e: e.dma_start(out=cstf[:], in_=cst), writes=[r_cstf], sem_owner=r_cstf)
        S.dma("sp", lambda e: e.dma_start(out=ppt[:], in_=pp), writes=[r_pp], sem_owner=r_pp)
        S.dma("pool", lambda e: e.dma_start(out=ident[:], in_=cstb[:, C_ID:C_ID + 128]), writes=[r_ident], sem_owner=r_ident)
        S.op("dve", lambda e: e.memset(onesb[:], 1.0), writes=[r_ones])
        S.op("dve", lambda e: e.memset(epst[:], EPS), writes=[r_eps])
        S.op("dve", lambda e: e.memset(mixinT[:, :, 0:128], 0.0), writes=[r_mix[k][0] for k in range(8)])
        for k in range(8):
            S.op("dve", lambda e, k=k: e.tensor_scalar(out=gB2[:, k, :], in0=onesb[:], scalar1=ppt[:, P_G2 + k:P_G2 + k + 1],
                                                       scalar2=None, op0=ALU.mult),
                 reads=[r_ones, r_pp], writes=[r_gB2])

        stat_res = {}

        def norm_stats(src_ap, r_src, key):
            col = norm_stats.n % 128
            norm_stats.n += 1
            rs = Res("stat%d" % norm_stats.n)
            stat_res[key] = rs
            junk = junks[norm_stats.n % 2]
            r_junk = r_junks[norm_stats.n % 2]
            S.op("act", lambda e: e.activation(out=junk[:], in_=src_ap, func=AF.Square, accum_out=stats[:, 0, col:col + 1]),
                 reads=(r_src if isinstance(r_src, list) else [r_src]), writes=[r_junk, rs])
            S.op("act", lambda e: e.activation(out=stats[:, 1, col:col + 1], in_=stats[:, 0, col:col + 1], func=AF.Ln,
                                               scale=1.0 / D, bias=epst[:, 0:1]), reads=[rs, r_eps], writes=[rs])
            S.op("act", lambda e: e.activation(out=stats[:, 2, col:col + 1], in_=stats[:, 1, col:col + 1], func=AF.Exp,
                                               scale=-0.5), reads=[rs], writes=[rs])
            return stats[:, 2, col:col + 1], rs

        norm_stats.n = 0

        def dump(name, src_ap, r_src):
            if name in dbg_out:
                S.dma("sp", lambda e: e.dma_start(out=dbg_out[name], in_=src_ap), reads=r_src, sem_owner=r_dbg)

        with ExitStack() as st:
            hT = T(st, "hT", [128, 8, TK], BF16)
            r_hT = [Res("hT%d" % t) for t in range(NT)]
            masks = T(st, "masks", [128, 3, 512], BF16); r_masks = Res("masks")
            met = T(st, "met", [128, 34], BF16); r_me = Res("me")
            gB1 = T(st, "gB1", [128, 8, 128], BF16); r_gB1 = Res("gB1")
            S.dma("pool", lambda e: e.dma_start(out=masks[:], in_=cstb[:, C_MASK:C_MASK + 1536].rearrange("p (a b) -> p a b", a=3)),
                  writes=[r_masks], sem_owner=r_masks)
            S.dma("pool", lambda e: e.dma_start(out=met[:], in_=cstb[:, C_ME:C_ME + 34]), writes=[r_me], sem_owner=r_me)
            for k in range(8):
                S.op("dve", lambda e, k=k: e.tensor_scalar(out=gB1[:, k, :], in0=onesb[:], scalar1=ppt[:, P_G1 + k:P_G1 + k + 1],
                                                           scalar2=None, op0=ALU.mult),
                     reads=[r_ones, r_pp], writes=[r_gB1])

            wpl = T(st, "wpl", [128, 8, 512], BF16); r_wpl = Res("wpl")
            wq = [T(st, "wq%d" % i, [128, 8, 384], BF16) for i in range(2)]
            r_wq = [Res("wq%d" % i) for i in range(2)]

            def load_wq(c):
                sl = c % 2
                for j in range(3):
                    S.dma("pool", lambda e, j=j, sl=sl, c=c: e.dma_start(
                        out=wq[sl][:, :, j * 128:(j + 1) * 128], in_=w_in_v[:, :, j * 512 + c * 128: j * 512 + (c + 1) * 128]),
                        writes=[r_wq[sl]], sem_owner=r_wq[sl])

            load_wq(0)
            load_wq(1)
            with ExitStack() as s1:
                NXI = 6
                xin = [T(s1, "xin%d" % i, [128, D], F32) for i in range(NXI)]
                r_xin = [Res("xin%d" % i) for i in range(NXI)]
                xs = [T(s1, "xs%d" % i, [128, D], BF16) for i in range(2)]
                r_xs = [Res("xs%d" % i) for i in range(2)]
                psT = [PS(s1, "psT%d" % i, [128, 8, 128], BF16) for i in range(2)]
                r_psT = [Res("psT%d" % i, psum=True) for i in range(2)]
                p1 = {}

                def p1_load_sq(t):
                    a = t % NXI
                    S.dma("sp", lambda e: e.dma_start(out=xin[a][:], in_=xk[t * 128:(t + 1) * 128, :]),
                          writes=[r_xin[a]], sem_owner=r_xin[a])
                    col = norm_stats.n % 128
                    norm_stats.n += 1
                    rs = Res("stat%d" % norm_stats.n)
                    junk = junks[norm_stats.n % 2]
                    r_junk = r_junks[norm_stats.n % 2]
                    S.op("act", lambda e: e.activation(out=junk[:], in_=xin[a][:], func=AF.Square, accum_out=stats[:, 0, col:col + 1]),
                         reads=[r_xin[a]], writes=[r_junk, rs])
                    p1[t] = (col, rs)

                def p1_rstd_scale(t):
                    a, b2 = t % NXI, t % 2
                    col, rs = p1[t]
                    S.op("act", lambda e: e.activation(out=stats[:, 1, col:col + 1], in_=stats[:, 0, col:col + 1], func=AF.Ln,
                                                       scale=1.0 / D, bias=epst[:, 0:1]), reads=[rs, r_eps], writes=[rs])
                    S.op("act", lambda e: e.activation(out=stats[:, 2, col:col + 1], in_=stats[:, 1, col:col + 1], func=AF.Exp,
                                                       scale=-0.5), reads=[rs], writes=[rs])
                    rstd = stats[:, 2, col:col + 1]
                    if t % 3 != 0:
                        S.op("dve", lambda e: e.tensor_scalar(out=xs[b2][:], in0=xin[a][:], scalar1=rstd, scalar2=None, op0=ALU.mult),
                             reads=[r_xin[a], rs], writes=[r_xs[b2]])
                    else:
                        S.op("act", lambda e: e.activation(out=xs[b2][:], in_=xin[a][:], func=AF.Copy, scale=rstd),
                             reads=[r_xin[a], rs], writes=[r_xs[b2]])
                    for k in range(8):
                        S.op("pe", lambda e, k=k: e.transpose(out=psT[b2][:, k, :], in_=xs[b2][:, k * 128:(k + 1) * 128], identity=ident[:]),
                             reads=[r_xs[b2], r_ident], writes=[r_psT[b2]], signal=(k == 7))

                def p1_evac(t):
                    b2 = t % 2
                    S.op("dve", lambda e: e.tensor_tensor(out=hT[:, :, t * 128:(t + 1) * 128], in0=psT[b2][:], in1=gB1[:], op=ALU.mult),
                         reads=[r_psT[b2], r_gB1], writes=[r_hT[t]])

                for i in range(NT + 2):
                    if i < NT:
                        p1_load_sq(i)
                    if 0 <= i - 2 < NT:
                        p1_evac(i - 2)
                    if 0 <= i - 1 < NT:
                        p1_rstd_scale(i - 1)
            snapA = S.snapshot()

            with ExitStack() as s2:
                qT = T(s2, "qT", [128, NQ], BF16); r_qT = Res("qT", snapA)
                kT = T(s2, "kT", [128, TK], BF16); r_kT = Res("kT", snapA)
                vT = T(s2, "vT", [128, TK], BF16); r_vT = Res("vT", snapA)
                Vc = T(s2, "Vc", [128, NB + 2, 130], BF16); r_Vc = Res("Vc", snapA)
                NPT = 8
                Pt = [T(s2, "Pt%d" % i, [128, 512], BF16) for i in range(NPT)]
                r_Pt = [Res("Pt%d" % i, snapA) for i in range(NPT)]
                acc = [T(s2, "acc%d" % i, [128, OWN + 8], F32) for i in range(2)]
                r_acc = [Res("acc%d" % i, snapA) for i in range(2)]
                rden = [T(s2, "rden%d" % i, [128, 512], F32) for i in range(2)]
                r_rden = [Res("rden%d" % i, snapA) for i in range(2)]
                Pe = [T(s2, "Pe%d" % i, [128, 34], BF16) for i in range(2)]
                r_Pe = [Res("Pe%d" % i, snapA) for i in range(2)]
                selA = cstf[0:65, C_SELA:C_SELA + 64]
                selB = cstf[:, C_SELB:C_SELB + 128]
                pj = [PS(s2, "pj%d" % i, [128, 512], F32) for i in range(2)]
                r_pj = [Res("pj%d" % i, snapA, psum=True) for i in range(2)]
                psS = [PS(s2, "psS%d" % i, [128, 512], F32) for i in range(4)]
                r_psS = [Res("psS%d" % i, snapA, psum=True) for i in range(4)]
                psO = [PS(s2, "psO%d" % i, [128, 512], F32) for i in range(2)]
                r_psO = [Res("psO%d" % i, snapA, psum=True) for i in range(2)]

                S.op("pool", lambda e: e.memset(Vc[:], 1.0), writes=[r_Vc])

                cast_jobs = [(plw_bf, pool_w, r_plwbf), (wout_bf, w_out, r_woutbf)]
                for i4 in range(4):
                    cast_jobs.append((wup_bf[256 * i4:256 * i4 + 256, :], w_up[256 * i4:256 * i4 + 256, :], r_wupbf))
                for i4 in range(2):
                    cast_jobs.append((wdn_bf[1408 * i4:1408 * i4 + 1408, :], w_down[1408 * i4:1408 * i4 + 1408, :], r_wdnbf))

                def issue_casts(n):
                    for _ in range(n):
                        if cast_jobs:
                            dst, src, rr = cast_jobs.pop(0)
                            S.dma("pool", lambda e, dst=dst, src=src: e.dma_start(out=dst, in_=src), writes=[rr], sem_owner=rr)
                pjn = [0]

                def proj_fm(wap_fn, r_w, tok0, ntok, dst_fn, r_dst, evac):
                    c0 = 0
                    while c0 < ntok:
                        n = min(512, ntok - c0)
                        sl = pjn[0] % 2
                        pjn[0] += 1
                        tiles = sorted(set(range((tok0 + c0) // 128, (tok0 + c0 + n - 1) // 128 + 1)))
                        for k in range(8):
                            S.op("pe", lambda e, k=k, sl=sl, c0=c0, n=n: e.matmul(
                                pj[sl][:, 0:n], lhsT=wap_fn(k), rhs=hT[:, k, tok0 + c0: tok0 + c0 + n],
                                start=(k == 0), stop=(k == 7)),
                                reads=[r_w] + [r_hT[t] for t in tiles], writes=[r_pj[sl]], signal=(k == 7))
                        evac(sl, c0, n)
                        c0 += n

                for c in range(4):
                    sl_w = c % 2
                    if 1 <= c and c + 1 < 4:
                        load_wq(c + 1)
                    issue_casts(3 if c == 0 else 2)
                    if c == 3:
                        S.dma("pool", lambda e: e.dma_start(out=wpl[:], in_=w_in_v[:, :, 1536:2048]), writes=[r_wpl], sem_owner=r_wpl)
                    evn = [0]

                    def evac_to(dst, r_dst, scale=None):
                        def f(sl, c0, n):
                            evn[0] += 1
                            if scale is not None:
                                S.op("act", lambda e: e.activation(out=dst[:, c0:c0 + n], in_=pj[sl][:, 0:n], func=AF.Copy, scale=scale),
                                     reads=[r_pj[sl]], writes=[r_dst])
                            elif evn[0] % 2 == 0:
                                S.op("act", lambda e: e.copy(out=dst[:, c0:c0 + n], in_=pj[sl][:, 0:n]),
                                     reads=[r_pj[sl]], writes=[r_dst])
                            else:
                                S.op("dve", lambda e: e.tensor_copy(out=dst[:, c0:c0 + n], in_=pj[sl][:, 0:n]),
                                     reads=[r_pj[sl]], writes=[r_dst])
                        return f

                    proj_fm(lambda k: wq[sl_w][:, k, 256:384], r_wq[sl_w], 0, TK, None, r_vT, evac_to(vT, r_vT))
                    proj_fm(lambda k: wq[sl_w][:, k, 128:256], r_wq[sl_w], 0, TK, None, r_kT, evac_to(kT, r_kT))
                    proj_fm(lambda k: wq[sl_w][:, k, 0:128], r_wq[sl_w], TK - NQ, NQ, None, r_qT, evac_to(qT, r_qT, scale=0.125))
                    def bfv(t_):
                        return t_.bitcast(BF16)[:].rearrange("p (a b) -> p a b", a=8)
                    vbanks = [(bfv(pj[i]), r_pj[i]) for i in range(2)] + [(bfv(psS[i]), r_psS[i]) for i in range(4)]
                    b0 = 0
                    gi = 0
                    while b0 < NB:
                        nb = min(8, NB - b0)
                        pv_, r_pv = vbanks[gi % 6]
                        for j in range(nb):
                            S.op("pe", lambda e, j=j, b0=b0, pv_=pv_: e.transpose(out=pv_[:, j, :], in_=vT[:, blocks[b0 + j]], identity=ident[:]),
                                 reads=[r_vT, r_ident], writes=[r_pv], signal=(j == nb - 1))
                        base = Vc[:, b0:b0 + nb, 0:64]
                        pa = [list(x) for x in base.ap]
                        dst = bass.AP(Vc, base.offset, [pa[0], pa[1], [66, 2], pa[2]])
                        src = pv_[:, 0:nb, :].rearrange("p n (h d) -> p n h d", h=2)
                        if gi % 2 == 0:
                            S.op("act", lambda e, dst=dst, src=src: e.copy(out=dst, in_=src), reads=[r_pv], writes=[r_Vc])
                        else:
                            S.op("dve", lambda e, dst=dst, src=src: e.tensor_copy(out=dst, in_=src), reads=[r_pv], writes=[r_Vc])
                        b0 += nb
                        gi += 1

                    HD = [dict(hs=slice(0, 64), M=65, vcols=slice(0, 65), ac=acc[0], r_ac=r_acc[0]),
                          dict(hs=slice(64, 128), M=128, vcols=slice(2, 130), ac=acc[1], r_ac=r_acc[1])]
                    groups = []
                    for g in range(4):
                        groups.append((0, [(0, 0, 4 * g + u) for u in range(4)], g))
                    for n in range(4):
                        groups.append((1, [(1, r, n) for r in range(4)], n))
                    for g in range(4):
                        groups.append((2, [(2, 4 * g + u, 0) for u in range(4)], g))
                    pairs = []
                    for gi2, (p, units, ga) in enumerate(groups):
                        pairs.append(units[0:2])
                        pairs.append(units[2:4])
                    dil = (1, 4, 16)
                    LAG = 2
                    npairs = len(pairs)
                    eb = [bidx[(2, r, -1)] for r in range(16)] + [bidx["hh"]]

                    def emit_pv(gi2, h):
                        hd = HD[h]
                        M = hd["M"]
                        ac, r_ac = hd["ac"], hd["r_ac"]
                        p, units, ga = groups[gi2]
                        for u, (pp_, r, n) in enumerate(units):
                            pi = 2 * gi2 + u // 2
                            ptn = (2 * pi + h) % NPT
                            pt, rpt = Pt[ptn], r_Pt[ptn]
                            bprev = bidx[(pp_, r, n - 1)]
                            bcur = bidx[(pp_, r, n)]
                            off = 256 * (u % 2)
                            S.op("pe", lambda e, u=u, pt=pt, bprev=bprev, off=off: e.matmul(
                                psO[h][0:M, 128 * u:128 * u + 128], lhsT=Vc[:, bprev, hd["vcols"]], rhs=pt[:, off:off + 128],
                                start=True, stop=False), reads=[r_Vc, rpt], writes=[r_psO[h]], signal=False)
                            S.op("pe", lambda e, u=u, pt=pt, bcur=bcur, off=off: e.matmul(
                                psO[h][0:M, 128 * u:128 * u + 128], lhsT=Vc[:, bcur, hd["vcols"]], rhs=pt[:, off + 128:off + 256],
                                start=False, stop=True), reads=[r_Vc, rpt], writes=[r_psO[h]], signal=(u == 3))
                        if p == 0:
                            dst = ac[0:M, 512 * ga:512 * ga + 512]
                            src = psO[h][0:M, :]
                        elif p == 1:
                            dst = ac[0:M, 512 * ga:512 * ga + 512].rearrange("p (i r) -> p r i", r=4)
                            src = psO[h][0:M, :].rearrange("p (u i) -> p u i", u=4)
                        else:
                            dst = ac[0:M, 0:OWN].rearrange("p (i r) -> p r i", r=16)[:, 4 * ga:4 * ga + 4, :]
                            src = psO[h][0:M, :].rearrange("p (u i) -> p u i", u=4)
                        if p == 0:
                            S.op("act", lambda e: e.copy(out=dst, in_=src), reads=[r_psO[h]], writes=[r_ac])
                        else:
                            S.op("dve", lambda e: e.tensor_tensor(out=dst, in0=src, in1=dst, op=ALU.add),
                                 reads=[r_psO[h], r_ac], writes=[r_ac])

                    for b, bi in enumerate(eb):
                        for h in range(2):
                            hs = HD[h]["hs"]
                            S.op("pe", lambda e, b=b, bi=bi, h=h, hs=hs: e.matmul(psS[h][:, 2 * b:2 * b + 2], lhsT=kT[hs, blocks[bi]], rhs=qT[hs, 126:128],
                                                                               start=True, stop=True),
                                 reads=[r_kT, r_qT], writes=[r_psS[h]], signal=(b == 16 and h == 1))
                    for h in range(2):
                        S.op("act", lambda e, h=h: e.activation(out=Pe[h][:], in_=psS[h][:, 0:34], func=AF.Exp), reads=[r_psS[h]], writes=[r_Pe[h]])
                        S.op("dve", lambda e, h=h: e.tensor_tensor(out=Pe[h][:], in0=Pe[h][:], in1=met[:], op=ALU.mult),
                             reads=[r_Pe[h], r_me], writes=[r_Pe[h]])

                    def e_pv(h):
                        hd = HD[h]
                        M = hd["M"]
                        for b, bi in enumerate(eb):
                            S.op("pe", lambda e, b=b, bi=bi: e.matmul(psO[h][0:M, 0:2], lhsT=Vc[:, bi, hd["vcols"]], rhs=Pe[h][:, 2 * b:2 * b + 2],
                                                                      start=(b == 0), stop=(b == 16)),
                                 reads=[r_Vc, r_Pe[h]], writes=[r_psO[h]], signal=(b == 16))
                        S.op("dve", lambda e: e.tensor_copy(out=hd["ac"][0:M, OWN:OWN + 2], in_=psO[h][0:M, 0:2]),
                             reads=[r_psO[h]], writes=[hd["r_ac"]])

                    for i in range(npairs + LAG):
                        if i < npairs:
                            units = pairs[i]
                            sb = 2 * (i % 2)
                            nmm = 0
                            for u, (pp_, r, n) in enumerate(units):
                                d = dil[pp_]
                                qs = QOFF + 128 * n * d + r
                                qsl = slice(qs, qs + 127 * d + 1, d)
                                kb = [blocks[bidx[(pp_, r, n - 1)]], blocks[bidx[(pp_, r, n)]]]
                                for wch in range(2):
                                    for h in range(2):
                                        hs = HD[h]["hs"]
                                        nmm += 1
                                        S.op("pe", lambda e, u=u, wch=wch, h=h, hs=hs, qsl=qsl, kb=kb: e.matmul(
                                            psS[sb + h][:, 256 * u + 128 * wch:256 * u + 128 * wch + 128], lhsT=kT[hs, kb[wch]], rhs=qT[hs, qsl],
                                            start=True, stop=True), reads=[r_kT, r_qT], writes=[r_psS[sb + h]], signal=(nmm == 8))
                            h0 = units[0][2] == 0
                            h1 = units[1][2] == 0
                            mk = 2 if (h0 and h1) else (1 if h0 else 0)
                            assert not (h1 and not h0)
                            for h in range(2):
                                pslot = (2 * i + h) % NPT
                                S.op("act", lambda e, h=h, pslot=pslot: e.activation(out=Pt[pslot][:], in_=psS[sb + h][:], func=AF.Exp),
                                     reads=[r_psS[sb + h]], writes=[r_Pt[pslot]])
                                meng = "pool" if (h == 0 or i % 3 == 0) else "dve"
                                S.op(meng, lambda e, pslot=pslot: e.tensor_tensor(out=Pt[pslot][:], in0=Pt[pslot][:], in1=masks[:, mk, :], op=ALU.mult),
                                     reads=[r_Pt[pslot], r_masks], writes=[r_Pt[pslot]])
                        if i == 1:
                            e_pv(0)
                            e_pv(1)
                        j = i - LAG
                        if j >= 0 and j % 2 == 1:
                            emit_pv(j // 2, 0)
                            emit_pv(j // 2, 1)

                    for h2 in range(2):
                        hs = HD[h2]["hs"]
                        ac, r_ac = HD[h2]["ac"], HD[h2]["r_ac"]
                        for j in range(5):
                            c0 = 512 * j
                            n = 512 if j < 4 else 2
                            sl = pjn[0] % 2
                            pjn[0] += 1
                            if h2 == 0:
                                S.op("pe", lambda e, sl=sl, c0=c0, n=n: e.matmul(pj[sl][0:64, 0:n], lhsT=selA, rhs=ac[0:65, c0:c0 + n],
                                                                                   start=True, stop=True),
                                     reads=[r_cstf, r_ac], writes=[r_pj[sl]])
                            else:
                                S.op("pe", lambda e, sl=sl, c0=c0, n=n: e.matmul(pj[sl][:, 0:n], lhsT=selB, rhs=ac[:, c0:c0 + n],
                                                                                   start=True, stop=True),
                                     reads=[r_cstf, r_ac], writes=[r_pj[sl]])
                            rd = rden[j % 2]
                            r_rd = r_rden[j % 2]
                            S.op("act", lambda e, sl=sl, n=n, rd=rd: e.activation(out=rd[hs, 0:n], in_=pj[sl][hs, 0:n], func=AF.Ln),
                                 reads=[r_pj[sl]], writes=[r_rd])
                            S.op("act", lambda e, n=n, rd=rd: e.activation(out=rd[hs, 0:n], in_=rd[hs, 0:n], func=AF.Exp, scale=-1.0),
                                 reads=[r_rd], writes=[r_rd])
                            dcol = (QOFF + c0) if j < 4 else 126
                            S.op("pool", lambda e, c0=c0, n=n, rd=rd, dcol=dcol: e.tensor_tensor(
                                out=mixinT[hs, c, dcol:dcol + n], in0=ac[hs, c0:c0 + n], in1=rd[hs, 0:n], op=ALU.mult),
                                reads=[r_ac, r_rd], writes=[r_mix[c][j + 1 if j < 4 else 0]])

            snapB = S.snapshot()
            with ExitStack() as s3:
                plw = T(s3, "plw", [128, 4, 128], BF16); r_plw = Res("plw", snapB)
                pin2 = [T(s3, "pin%d" % i, [128, NQ], F32) for i in range(2)]
                r_pin2 = [Res("pin%d" % i, snapB) for i in range(2)]
                tA2 = [T(s3, "tA%d" % i, [128, NQ], F32) for i in range(2)]
                r_tA2 = [Res("tA%d" % i, snapB) for i in range(2)]
                tB2 = [T(s3, "tB%d" % i, [128, NQ], F32) for i in range(2)]
                r_tB2 = [Res("tB%d" % i, snapB) for i in range(2)]
                t162 = [T(s3, "t16_%d" % i, [128, 16], F32) for i in range(2)]
                r_t162 = [Res("t16_%d" % i, snapB) for i in range(2)]
                pld2 = [T(s3, "pld%d" % i, [128, NQ], BF16) for i in range(2)]
                r_pld2 = [Res("pld%d" % i, snapB) for i in range(2)]
                pj = [PS(s3, "pjp%d" % i, [128, 512], F32) for i in range(3)]
                r_pj = [Res("pjp%d" % i, snapB, psum=True) for i in range(3)]
                po = [PS(s3, "pop%d" % i, [128, 512], F32) for i in range(2)]
                r_po = [Res("pop%d" % i, snapB, psum=True) for i in range(2)]
                S.dma("sp", lambda e: e.dma_start(out=plw[:], in_=plw_bf_v), reads=[r_plwbf], writes=[r_plw], sem_owner=r_plw)
                tok0 = TK - NQ
                pjn = [0]

                order = [2, 3, 1, 0]
                slot_of = {g_: i_ % 2 for i_, g_ in enumerate(order)}

                def pool_P(g):
                    pin, r_pin = pin2[slot_of[g]], r_pin2[slot_of[g]]
                    c0 = 0
                    while c0 < NQ:
                        n = min(512, NQ - c0)
                        sl = pjn[0] % 3
                        pjn[0] += 1
                        tiles = sorted(set(range((tok0 + c0) // 128, (tok0 + c0 + n - 1) // 128 + 1)))
                        for k in range(8):
                            S.op("pe", lambda e, k=k, sl=sl, c0=c0, n=n, g=g: e.matmul(
                                pj[sl][:, 0:n], lhsT=wpl[:, k, 128 * g:128 * g + 128], rhs=hT[:, k, tok0 + c0: tok0 + c0 + n],
                                start=(k == 0), stop=(k == 7)),
                                reads=[r_wpl] + [r_hT[t] for t in tiles], writes=[r_pj[sl]], signal=(k == 7))
                        S.op("act", lambda e, sl=sl, c0=c0, n=n: e.copy(out=pin[:, c0:c0 + n], in_=pj[sl][:, 0:n]),
                             reads=[r_pj[sl]], writes=[r_pin])
                        c0 += n

                def pool_E(g):
                    w = 2 << g
                    pin, r_pin = pin2[slot_of[g]], r_pin2[slot_of[g]]
                    pld, r_pld = pld2[slot_of[g]], r_pld2[slot_of[g]]
                    t16, r_t16 = t162[slot_of[g]], r_t162[slot_of[g]]
                    src, r_src = pin, r_pin
                    bufs = [(tA2[slot_of[g]], r_tA2[slot_of[g]]), (tB2[slot_of[g]], r_tB2[slot_of[g]])]
                    step = 1
                    lvl = 0
                    while step < w:
                        dst, r_dst = bufs[lvl % 2]
                        lo = 16 * (lvl + 1)
                        eng = "pool" if (lvl + g) % 2 == 0 else "dve"
                        S.op(eng, lambda e, dst=dst, src=src, lo=lo, step=step: e.tensor_tensor(
                            out=dst[:, lo:NQ], in0=src[:, lo:NQ], in1=src[:, lo - step:NQ - step], op=ALU.add),
                            reads=[r_src], writes=[r_dst])
                        src, r_src = dst, r_dst
                        step *= 2
                        lvl += 1
                    S.op("dve", lambda e, src=src, w=w: e.scalar_tensor_tensor(out=pld[:, 126:NQ], in0=src[:, 126:NQ], scalar=1.0 / w,
                                                                             in1=pin[:, 126:NQ], op0=ALU.mult, op1=ALU.subtract),
                         reads=[r_src, r_pin], writes=[r_pld])
                    S.op("pool", lambda e, src=src, g=g: e.tensor_tensor(out=t16[:], in0=src[:, 128:144],
                                                                        in1=cstf[:, C_INVC + 16 * g:C_INVC + 16 * g + 16], op=ALU.mult),
                         reads=[r_src, r_cstf], writes=[r_t16])
                    S.op("pool", lambda e: e.tensor_tensor(out=pld[:, 128:144], in0=t16[:], in1=pin[:, 128:144], op=ALU.subtract),
                         reads=[r_t16, r_pin, r_pld], writes=[r_pld])

                def pool_M(g):
                    pld, r_pld = pld2[slot_of[g]], r_pld2[slot_of[g]]
                    for j in range(5):
                        if j < 4:
                            c0, n = QOFF + 512 * j, 512
                        else:
                            c0, n = 126, 2
                        sl = j % 2
                        S.op("pe", lambda e, sl=sl, c0=c0, n=n, g=g: e.matmul(po[sl][:, 0:n], lhsT=plw[:, g, :], rhs=pld[:, c0:c0 + n],
                                                                              start=True, stop=True),
                             reads=[r_plw, r_pld], writes=[r_po[sl]])
                        S.op("act", lambda e, sl=sl, c0=c0, n=n, g=g: e.activation(out=mixinT[:, 4 + g, c0:c0 + n], in_=po[sl][:, 0:n],
                                                                                   func=AF.Copy, scale=ppt[:, P_PS + g:P_PS + g + 1]),
                             reads=[r_po[sl], r_pp], writes=[r_mix[4 + g][j + 1 if j < 4 else 0]])

                for step_ in range(5):
                    if step_ < 4:
                        pool_P(order[step_])
                        pool_E(order[step_])
                    if step_ >= 1:
                        pool_M(order[step_ - 1])
        snapC = S.snapshot()
        if "mixinT" in dbg_out:
            dump("mixinT", mixinT[:].rearrange("p a b -> p (a b)"), [r for rr in r_mix for r in rr])

        with ExitStack() as st:
            gpostB = T(st, "gpostB", [128, D], F32); r_gpost = Res("gpost", snapC)
            gfpostB = T(st, "gfpostB", [128, D], F32); r_gfpost = Res("gfpost", snapC)
            wo = T(st, "wo", [128, 8, D], BF16); r_wo = Res("wo", snapC)
            wd = T(st, "wd", [128, NFC, D], BF16); r_wd = Res("wd", snapC)
            NWU = 3
            wu = [T(st, "wu%d" % i, [128, 8, 2, 256], BF16) for i in range(NWU)]
            r_wu = [Res("wu%d" % i, snapC) for i in range(NWU)]
            x1 = T(st, "x1", [128, 4, D], F32)
            r_x1 = [Res("x1_%d" % i, snapC) for i in range(4)]
            xr = [T(st, "xr%d" % i, [128, D], F32) for i in range(2)]
            r_xr = [Res("xr%d" % i, snapC) for i in range(2)]
            r_xo = [Res("xo%d" % i) for i in range(2)]
            xs2 = [T(st, "xs2_%d" % i, [128, D], BF16) for i in range(2)]
            r_xs2 = [Res("xs2_%d" % i, snapC) for i in range(2)]
            h2T = T(st, "h2T", [128, 8, 512], BF16)
            r_h2 = [Res("h2_%d" % i, snapC) for i in range(4)]
            h2e = T(st, "h2e", [128, 8, 2], BF16); r_h2e = Res("h2e", snapC)
            yT = T(st, "yT", [128, NFC, 512], BF16)
            r_yT = [Res("yT%d" % i, snapC) for i in range(NFC)]
            x1e = yT.bitcast(F32)[:, 0:4, :].rearrange("p a b -> p (a b)")
            r_x1e = r_yT[0:4]
            cs = [[T(st, "cs%d_%d" % (p_, i), [128, 512], F32) for i in range(2)] for p_ in range(2)]
            r_cs = [[Res("cs%d_%d" % (p_, i), snapC) for i in range(2)] for p_ in range(2)]
            gl = [T(st, "gl%d" % i, [128, 512], BF16) for i in range(2)]
            r_gl = [Res("gl%d" % i, snapC) for i in range(2)]
            uprev = T(st, "uprev", [128, 2 * NFC, 2], F32)
            r_up = [Res("uprev%d" % i, snapC) for i in range(2 * NFC)]
            corr = T(st, "corr", [128, 2 * NFC, 2], F32)
            r_corr = [Res("corr%d" % i, snapC) for i in range(2 * NFC)]
            tb = T(st, "tb", [128, 4], F32)
            r_tb = [Res("tb%d" % i, snapC) for i in range(4)]
            tbn = [0]

            def make_corr(ci):
                j = tbn[0] % 4
                tbn[0] += 1
                S.op("pool", lambda e: e.tensor_tensor(out=corr[:, ci, :], in0=uprev[:, ci, :],
                                                       in1=ppt[:, P_CW2 + 3 * ci:P_CW2 + 3 * ci + 2], op=ALU.mult),
                     reads=[r_up[ci], r_pp], writes=[r_corr[ci]])
                S.op("pool", lambda e: e.tensor_tensor(out=tb[:, j:j + 1], in0=uprev[:, ci, 1:2],
                                                       in1=ppt[:, P_CW2 + 3 * ci + 2:P_CW2 + 3 * ci + 3], op=ALU.mult),
                     reads=[r_up[ci], r_pp], writes=[r_tb[j]])
                S.op("pool", lambda e: e.tensor_tensor(out=corr[:, ci, 0:1], in0=corr[:, ci, 0:1], in1=tb[:, j:j + 1], op=ALU.add),
                     reads=[r_corr[ci], r_tb[j]], writes=[r_corr[ci]])
            pm = [PS(st, "pm%d" % i, [128, D], F32) for i in range(2)]
            r_pmh = [[Res("pm%d_%d" % (i, h), snapC, psum=True) for h in range(2)] for i in range(2)]
            r_pm = [r_pmh[0], r_pmh[1]]
            pu_t = [[PS(st, "pu%d_%d" % (p_, i), [128, 512], F32) for i in range(2)] for p_ in range(2)]
            r_pu_t = [[Res("pu%d_%d" % (p_, i), snapC, psum=True) for i in range(2)] for p_ in range(2)]
            pu_slots = [[(pu_t[p_][0][:], r_pu_t[p_][0]), (pu_t[p_][1][:], r_pu_t[p_][1]),
                         (pm[1][:, 512 * p_:512 * p_ + 512], r_pmh[1][p_]), (pm[0][:, 512 * p_:512 * p_ + 512], r_pmh[0][p_])]
                        for p_ in range(2)]
            ptr = pu_t[0][0].bitcast(BF16)[:].rearrange("p (a b) -> p a b", a=8)
            r_ptr = r_pu_t[0][0]
            pue = pm[0]

            S.dma("sp", lambda e: e.dma_start(out=wo[:], in_=wout_bf_v), reads=[r_woutbf], writes=[r_wo], sem_owner=r_wo)
            S.dma("sp", lambda e: e.dma_start(out=gpostB[:], in_=g_post.partition_broadcast(128)),
                  writes=[r_gpost], sem_owner=r_gpost)
            S.dma("sp", lambda e: e.dma_start(out=gfpostB[:], in_=g_fpost.partition_broadcast(128)),
                  writes=[r_gfpost], sem_owner=r_gfpost)
            NG = NFC // 2
            wu_loaded = [0]

            def load_wu_next():
                n = wu_loaded[0]
                if n >= 4 * NG:
                    return
                wu_loaded[0] += 1
                g2 = n % NG
                sl = n % NWU
                for part in range(2):
                    col = part * DFF + 256 * g2
                    S.dma("sp", lambda e, sl=sl, part=part, col=col: e.dma_start(out=wu[sl][:, :, part, :], in_=wup_bf_v[:, :, col:col + 256]),
                          reads=[r_wupbf], writes=[r_wu[sl]], sem_owner=r_wu[sl])

            tcnt = [0]

            def stage_m(tiles):
                n = len(tiles)
                st_ = [dict() for _ in range(n)]

                def S1(j):
                    mcol, xrow, x1_ap, r_x1t, h2_fn = tiles[j]
                    i = tcnt[0]
                    tcnt[0] += 1
                    sl = i % 2
                    st_[j]["sl"] = sl
                    S.dma("sp", lambda e: e.dma_start(out=xr[sl][:], in_=xk[xrow:xrow + 128, :]), writes=[r_xr[sl]], sem_owner=r_xr[sl])
                    jblk = 0 if mcol < QOFF else 1 + (mcol - QOFF) // 512
                    for half in range(2):
                        for k in range(8):
                            S.op("pe", lambda e, half=half, k=k: e.matmul(pm[sl][:, 512 * half:512 * half + 512], lhsT=mixinT[:, k, mcol:mcol + 128],
                                                                          rhs=wo[:, k, 512 * half:512 * half + 512], start=(k == 0), stop=(k == 7)),
                                 reads=[r_mix[k][jblk], r_wo], writes=r_pm[sl], signal=(half == 1 and k == 7))
                    rstd, rs = norm_stats(pm[sl][:], r_pm[sl], ("ma", mcol))
                    S.op("dve", lambda e: e.scalar_tensor_tensor(out=x1_ap, in0=pm[sl][:], scalar=rstd, in1=gpostB[:], op0=ALU.mult, op1=ALU.mult),
                         reads=r_pm[sl] + [rs, r_gpost], writes=r_x1t)
                    S.op("dve", lambda e: e.tensor_tensor(out=x1_ap, in0=x1_ap, in1=xr[sl][:], op=ALU.add),
                         reads=r_x1t + [r_xr[sl]], writes=r_x1t)

                def S2(j):
                    mcol, xrow, x1_ap, r_x1t, h2_fn = tiles[j]
                    sl = st_[j]["sl"]
                    rstd2, rs2 = norm_stats(x1_ap, r_x1t, ("mb", mcol))
                    S.op("act", lambda e: e.activation(out=xs2[sl][:], in_=x1_ap, func=AF.Copy, scale=rstd2),
                         reads=r_x1t + [rs2], writes=[r_xs2[sl]])
                    for k in range(8):
                        S.op("pe", lambda e, k=k: e.transpose(out=ptr[:, k, :], in_=xs2[sl][:, 128 * k:128 * k + 128], identity=ident[:]),
                             reads=[r_xs2[sl], r_ident], writes=[r_ptr], signal=(k == 7))

                def S3(j):
                    tiles[j][4]()

                for step in range(n + 3):
                    if 0 <= step - 3 < n:
                        S3(step - 3)
                    if step < n:
                        S1(step)
                    if 0 <= step - 2 < n:
                        S2(step - 2)

            def h2_evac_e():
                S.op("dve", lambda e: e.tensor_tensor(out=h2e[:], in0=ptr[:, :, 126:128], in1=gB2[:, :, 126:128], op=ALU.mult),
                     reads=[r_ptr, r_gB2], writes=[r_h2e])

            def mk_h2_evac(t):
                def f():
                    S.op("dve", lambda e: e.tensor_tensor(out=h2T[:, :, 128 * t:128 * t + 128], in0=ptr, in1=gB2[:], op=ALU.mult),
                         reads=[r_ptr, r_gB2], writes=[r_h2[t]])
                return f

            gidx = [0]
            for b in range(4):
                tiles = []
                if b == 0:
                    tiles.append((0, 2048, x1e, r_x1e, h2_evac_e))
                for t in range(4):
                    tiles.append((QOFF + 512 * b + 128 * t, OWN0 + 512 * b + 128 * t, x1[:, t, :], [r_x1[t]], mk_h2_evac(t)))
                stage_m(tiles)
                if b == 0:
                    load_wu_next()
                    load_wu_next()
                    for k2 in range(2):
                        S.dma("sp", lambda e, k2=k2: e.dma_start(out=wd[:, 11 * k2:11 * k2 + 11, :], in_=wdn_bf_v[:, 11 * k2:11 * k2 + 11, :]),
                              reads=[r_wdnbf], writes=[r_wd], sem_owner=r_wd)
                    dump("x1", x1[:].rearrange("p a b -> p (a b)"), r_x1)
                for i in range(NFC + 1):
                    if i < NFC:
                        fc = i
                        f2 = fc % 2
                        sl2 = fc % 2
                        nsl = 3 if b == 0 else 4
                        pus = [pu_slots[part][fc % nsl] for part in range(2)]
                        if f2 == 0:
                            slw = gidx[0] % NWU
                            gidx[0] += 1
                            load_wu_next()
                        cis = [part * NFC + fc for part in range(2)]
                        for part in range(2):
                            ps_, r_ps = pus[part]
                            for k in range(8):
                                S.op("pe", lambda e, k=k, part=part, f2=f2, ps_=ps_, slw=slw: e.matmul(
                                    ps_, lhsT=wu[slw][:, k, part, 128 * f2:128 * f2 + 128], rhs=h2T[:, k, :],
                                    start=(k == 0), stop=(k == 7)),
                                    reads=[r_wu[slw]] + r_h2, writes=[r_ps], signal=(k == 7))
                            if b == 0:
                                ci = cis[part]
                                for k in range(8):
                                    S.op("pe", lambda e, k=k, part=part, f2=f2, ci=ci, slw=slw: e.matmul(
                                        pue[:, 512 * part + 2 * fc:512 * part + 2 * fc + 2], lhsT=wu[slw][:, k, part, 128 * f2:128 * f2 + 128], rhs=h2e[:, k, :],
                                        start=(k == 0), stop=(k == 7)),
                                        reads=[r_wu[slw], r_h2e], writes=[r_pmh[0][part]], signal=(k == 7))
                                S.op("act", lambda e, ci=ci, part=part: e.copy(out=uprev[:, ci, :], in_=pue[:, 512 * part + 2 * fc:512 * part + 2 * fc + 2]),
                                     reads=[r_pmh[0][part]], writes=[r_up[ci]])
                                make_corr(ci)
                        cwf = lambda ci, j: ppt[:, P_CW + 3 * ci + j:P_CW + 3 * ci + j + 1]
                        for part in range(2):
                            ci = cis[part]
                            S.op("act", lambda e, part=part, ci=ci: e.activation(
                                out=cs[part][sl2][:], in_=pus[part][0], func=AF.Identity, scale=cwf(ci, 2), bias=ppt[:, P_CB + ci:P_CB + ci + 1]),
                                reads=[pus[part][1], r_pp], writes=[r_cs[part][sl2]])
                            if b < 3:
                                S.op("act", lambda e, part=part, ci=ci: e.copy(out=uprev[:, ci, :], in_=pus[part][0][:, 510:512]),
                                     reads=[pus[part][1]], writes=[r_up[ci]])
                        for part in range(2):
                            ci = cis[part]
                            S.op("dve", lambda e, part=part, ci=ci: e.scalar_tensor_tensor(
                                out=cs[part][sl2][:, 1:512], in0=pus[part][0][:, 0:511], scalar=cwf(ci, 1), in1=cs[part][sl2][:, 1:512],
                                op0=ALU.mult, op1=ALU.add),
                                reads=[pus[part][1], r_pp, r_cs[part][sl2]], writes=[r_cs[part][sl2]])
                        for part in range(2):
                            ci = cis[part]
                            S.op("dve", lambda e, part=part, ci=ci: e.scalar_tensor_tensor(
                                out=cs[part][sl2][:, 2:512], in0=pus[part][0][:, 0:510], scalar=cwf(ci, 0), in1=cs[part][sl2][:, 2:512],
                                op0=ALU.mult, op1=ALU.add),
                                reads=[pus[part][1], r_pp, r_cs[part][sl2]], writes=[r_cs[part][sl2]])
                    if i >= 1:
                        fc = i - 1
                        sl2 = fc % 2
                        for part in range(2):
                            ci = part * NFC + fc
                            S.op("pool", lambda e, part=part, ci=ci, sl2=sl2: e.tensor_tensor(
                                out=cs[part][sl2][:, 0:2], in0=cs[part][sl2][:, 0:2], in1=corr[:, ci, :], op=ALU.add),
                                reads=[r_cs[part][sl2], r_corr[ci]], writes=[r_cs[part][sl2]])
                        S.op("act", lambda e, sl2=sl2: e.activation(out=gl[sl2][:], in_=cs[0][sl2][:], func=AF.Gelu_apprx_tanh),
                             reads=[r_cs[0][sl2]], writes=[r_gl[sl2]])
                        S.op("pool", lambda e, fc=fc, sl2=sl2: e.tensor_tensor(out=yT[:, fc, :], in0=gl[sl2][:], in1=cs[1][sl2][:], op=ALU.mult),
                             reads=[r_gl[sl2], r_cs[1][sl2]], writes=[r_yT[fc]])
                        if b < 3:
                            for part in range(2):
                                make_corr(part * NFC + fc)
                for t in range(4):
                    sl = tcnt[0] % 2
                    tcnt[0] += 1
                    extra = []
                    for half in range(2):
                        for fc in range(NFC):
                            S.op("pe", lambda e, half=half, fc=fc, t=t: e.matmul(
                                pm[sl][:, 512 * half:512 * half + 512], lhsT=yT[:, fc, 128 * t:128 * t + 128],
                                rhs=wd[:, fc, 512 * half:512 * half + 512], start=(fc == 0), stop=(fc == NFC - 1)),
                                reads=[r_yT[fc], r_wd], writes=r_pm[sl], signal=(half == 1 and fc == NFC - 1))
                    rstd3, rs3 = norm_stats(pm[sl][:], r_pm[sl], ("f", b, t))
                    S.op("dve", lambda e, t=t: e.scalar_tensor_tensor(out=xr[sl][:], in0=pm[sl][:], scalar=rstd3, in1=gfpostB[:],
                                                                    op0=ALU.mult, op1=ALU.mult),
                         reads=r_pm[sl] + [rs3, r_gfpost], writes=[r_xr[sl]])
                    S.op("pool", lambda e, t=t: e.tensor_tensor(out=xr[sl][:], in0=xr[sl][:], in1=x1[:, t, :], op=ALU.add),
                         reads=[r_xr[sl], r_x1[t]], writes=[r_xr[sl]])
                    row = 512 * b + 128 * t
                    S.dma("sp", lambda e, row=row: e.dma_start(out=out[row:row + 128, :], in_=xr[sl][:]),
                          reads=[r_xr[sl]], sem_owner=r_xo[sl])

            toks = [(r_xo[0].dsem, r_xo[0].dcnt), (r_xo[1].dsem, r_xo[1].dcnt)]
            if r_dbg.dsem is not None:
                toks.append((r_dbg.dsem, r_dbg.dcnt))
            S.wait_tokens("sp", toks)
        for E in S.ENG:
            assert not S.pending[E], E
    return nc


def _host_consts(j):
    q0 = j * OWN
    c = np.zeros((128, CST_COLS), np.float32)
    cb = np.zeros((128, CSTB_COLS), np.float32)
    cb[:, C_ID:C_ID + 128] = np.eye(128, dtype=np.float32)
    ik = np.arange(128)[:, None]
    iq = np.arange(128)[None, :]
    mprev = (ik >= iq).astype(np.float32)
    mcur = (ik <= iq).astype(np.float32)
    mhalo = mprev if j > 0 else np.zeros_like(mprev)
    kinds = [(mprev, mprev), (mhalo, mprev), (mhalo, mhalo)]
    for m, (pa, pb) in enumerate(kinds):
        cb[:, C_MASK + 512 * m:C_MASK + 512 * (m + 1)] = np.concatenate([pa, mcur, pb, mcur], axis=1)
    me = np.zeros((128, 17, 2), np.float32)
    for b in range(17):
        i = np.arange(128)
        if b < 16:
            pk = q0 - 2048 + 16 * i + b
        else:
            pk = q0 - 2176 + i
        for e in range(2):
            pq = q0 - 2 + e
            diff = pq - pk
            cnt = np.zeros(128, np.float32)
            for d in (1, 4, 16):
                cnt += ((diff >= 0) & (diff % d == 0) & (diff // d <= 128)).astype(np.float32)
            if j > 0:
                cnt *= (pk >= 0)
            me[:, b, e] = cnt
    cb[:, C_ME:C_ME + 34] = me.reshape(128, 34)
    c[64, C_SELA:C_SELA + 64] = 1.0
    c[62, C_SELB + 64:C_SELB + 128] = 1.0
    for g in range(4):
        w = 2 << g
        pos = q0 + np.arange(16)
        c[:, C_INVC + 16 * g:C_INVC + 16 * g + 16] = (1.0 / np.minimum(pos + 1, w))[None, :]
    return c, cb


_NC_CACHE = {}


def kernel(x, g_mix_pre, w_in, pool_w, pool_scale, w_out, g_mix_post, g_ffn_pre, w_up, conv_w, conv_b, w_down,
           g_ffn_post, _dbg=None):
    f = lambda a: np.ascontiguousarray(np.asarray(a, dtype=np.float32))
    x = f(x)
    B = x.shape[0]
    pp = np.zeros((128, PP_COLS), np.float32)
    pp[:, P_G1:P_G1 + 8] = f(g_mix_pre)[0].reshape(8, 128).T
    pp[:, P_G2:P_G2 + 8] = f(g_ffn_pre)[0].reshape(8, 128).T
    pp[:, P_PS:P_PS + 4] = f(pool_scale)[0].reshape(4, 128).T
    cw = f(conv_w)[0].reshape(3, 44, 128)
    pp[:, P_CW:P_CW + 132] = cw.transpose(2, 1, 0).reshape(128, 132)
    pp[:, P_CB:P_CB + 44] = f(conv_b)[0].reshape(44, 128).T
    cw2 = np.stack([cw[0], cw[0], cw[1]], axis=0)
    pp[:, P_CW2:P_CW2 + 132] = cw2.transpose(2, 1, 0).reshape(128, 132)
    shared = {
        "pp": pp,
        "w_in": f(w_in)[0], "pool_w": f(pool_w)[0].reshape(512, 128), "w_out": f(w_out)[0],
        "w_up": f(w_up)[0], "w_down": f(w_down)[0],
        "g_mix_post": f(g_mix_post)[0], "g_ffn_post": f(g_ffn_post)[0],
    }
    in_maps = []
    for c in range(NCORES):
        b, j = divmod(c, 4)
        q0 = j * OWN
        xkc = np.zeros((TK, D), np.float32)
        lo = q0 - OWN0
        s = max(lo, 0)
        xkc[s - lo:, :] = x[b, s:q0 + OWN, :]
        m = dict(shared)
        m["xk"] = xkc
        m["cst"], m["cstb"] = _host_consts(j)
        in_maps.append(m)
    key = repr(_dbg)
    if key not in _NC_CACHE:
        _NC_CACHE[key] = build_nc(_dbg)
    nc = _NC_CACHE[key]
    res = run_bass_kernel_spmd(nc, in_maps, core_ids=list(range(NCORES)))
    outs = [np.asarray(r["out"], dtype=np.float32) for r in res.results]
    full = np.stack(outs).reshape(B, SEQ, D)
    if _dbg:
        return full, res.results
    return full
```
